# Optimizing a Trainium2 kernel written in Bass

```python
import jax, jax.numpy as jnp
from jax import lax
import numpy as np

D_MODEL = 2048
BATCH = 8
SEQ = 2048
DEPTH = 4
DEC_BATCH = 16
DEC_SEQ = 64
PAST_LEN = 4096

CHUNK = 64
N_LEFT_CHUNKS = 8
ATT_WIN = N_LEFT_CHUNKS * CHUNK
CONV_W = 31
D_CONV = D_MODEL // 2
HEAD_DIM = 128
N_ATT_HEADS = 4
D_ATT = N_ATT_HEADS * HEAD_DIM
N_MEM_HEADS = 4
D_MEM = N_MEM_HEADS * HEAD_DIM
N_MEM = 256
REL_CLIP = 128
D_MIX = D_CONV + D_ATT + D_MEM
IN_WIDTHS = (D_CONV, D_CONV, D_CONV, D_ATT, D_ATT, D_ATT, D_ATT, D_MEM, D_MEM)
D_IN = sum(IN_WIDTHS)
ALPHA = (2 * DEPTH) ** 0.25
BETA = (8 * DEPTH) ** -0.25
LN_EPS = 1e-5

kernel_name = "hymba_streaming_conformer_step"


def layer_norm(x, g, b):
    xf = x.astype(jnp.float32)
    mu = jnp.mean(xf, axis=-1, keepdims=True)
    var = jnp.mean(jnp.square(xf - mu), axis=-1, keepdims=True)
    return ((xf - mu) * lax.rsqrt(var + LN_EPS) * g.astype(jnp.float32) + b.astype(jnp.float32)).astype(x.dtype)


def split_in_proj(h):
    idx = [int(s) for s in np.cumsum(IN_WIDTHS)[:-1]]
    return jnp.split(h, idx, axis=-1)


def conv_branch(a, b, left, conv_w, conv_b, ln_g, ln_b, w_pw):
    u = a * jax.nn.sigmoid(b)
    up = jnp.concatenate([left.astype(u.dtype), u], axis=1)
    y = lax.conv_general_dilated(up, conv_w[:, None, :].astype(u.dtype), window_strides=(1,), padding='VALID',
                                 dimension_numbers=('NWC', 'WIO', 'NWC'), feature_group_count=D_CONV)
    y = layer_norm(y + conv_b, ln_g, ln_b)
    y = jax.nn.silu(y) @ w_pw
    return y, up[:, -(CONV_W - 1):]


def rel_position_bias(table, q_pos, k_pos):
    d = jnp.clip(q_pos[:, None] - k_pos[None, :], -REL_CLIP, REL_CLIP) + REL_CLIP
    return table[:, d].astype(jnp.float32)


def chunk_attention_prompt(q, k, v, table):
    B, T, H, Dh = q.shape
    nc = T // CHUNK
    qc = q.reshape(B, nc, CHUNK, H, Dh)
    pad = jnp.zeros((B, ATT_WIN, H, Dh), k.dtype)
    kp = jnp.concatenate([pad, k], axis=1).reshape(B, nc + N_LEFT_CHUNKS, CHUNK, H, Dh)
    vp = jnp.concatenate([pad, v], axis=1).reshape(B, nc + N_LEFT_CHUNKS, CHUNK, H, Dh)
    kb = jnp.concatenate([kp[:, i:i + nc] for i in range(N_LEFT_CHUNKS + 1)], axis=2)
    vb = jnp.concatenate([vp[:, i:i + nc] for i in range(N_LEFT_CHUNKS + 1)], axis=2)
    band = ATT_WIN + CHUNK
    bias = rel_position_bias(table, jnp.arange(CHUNK) + ATT_WIN, jnp.arange(band))
    valid = (jnp.arange(nc)[:, None] * CHUNK + jnp.arange(band)[None, :]) >= ATT_WIN
    s = jnp.einsum('bcqhd,bckhd->bchqk', qc, kb, preferred_element_type=jnp.float32) * (HEAD_DIM ** -0.5)
    s = jnp.where(valid[None, :, None, None, :], s + bias[None, None], -1e30)
    p = jax.nn.softmax(s, axis=-1)
    o = jnp.einsum('bchqk,bckhd->bcqhd', p.astype(v.dtype), vb)
    return o.reshape(B, T, H * Dh)


def chunk_attention_sample(q, k, v, k_cache, v_cache, table):
    B, T, H, Dh = q.shape
    L = k_cache.shape[1]
    kk = jnp.concatenate([k_cache.astype(k.dtype), k], axis=1)
    vv = jnp.concatenate([v_cache.astype(v.dtype), v], axis=1)
    bias = rel_position_bias(table, jnp.arange(T), jnp.arange(L + T) - L)
    s = jnp.einsum('bqhd,bkhd->bhqk', q, kk, preferred_element_type=jnp.float32) * (HEAD_DIM ** -0.5)
    p = jax.nn.softmax(s + bias[None], axis=-1)
    o = jnp.einsum('bhqk,bkhd->bqhd', p.astype(vv.dtype), vv)
    return o.reshape(B, T, H * Dh)


def memory_attention(q, mk, mv):
    B, T, H, Dh = q.shape
    s = jnp.einsum('bthd,bnhd->bhtn', q, mk.astype(q.dtype), preferred_element_type=jnp.float32) * (HEAD_DIM ** -0.5)
    p = jax.nn.softmax(s, axis=-1)
    o = jnp.einsum('bhtn,bnhd->bthd', p.astype(q.dtype), mv.astype(q.dtype))
    return o.reshape(B, T, H * Dh)


def layer_step(x, mk, mv, conv_left, k_cache, v_cache, w_in, conv_w, conv_b, conv_ln_g, conv_ln_b,
               w_pw, rel_table, w_out, ln_g, ln_b):
    B, T, _ = x.shape
    ca, cb, cg, q, k, v, ag, mq, mg = split_in_proj(x @ w_in)
    conv_out, conv_state = conv_branch(ca, cb, conv_left, conv_w, conv_b, conv_ln_g, conv_ln_b, w_pw)
    q = q.reshape(B, T, N_ATT_HEADS, HEAD_DIM)
    k = k.reshape(B, T, N_ATT_HEADS, HEAD_DIM)
    v = v.reshape(B, T, N_ATT_HEADS, HEAD_DIM)
    if k_cache is None:
        att = chunk_attention_prompt(q, k, v, rel_table)
    else:
        att = chunk_attention_sample(q, k, v, k_cache, v_cache, rel_table)
    mem = memory_attention(mq.reshape(B, T, N_MEM_HEADS, HEAD_DIM), mk, mv)
    mix = jnp.concatenate([conv_out * jax.nn.silu(cg), att * jax.nn.silu(ag), mem * jax.nn.silu(mg)], axis=-1)
    y = layer_norm(ALPHA * x + mix @ w_out, ln_g, ln_b)
    return y, conv_state, k, v


def setup_inputs(seed: int = 0) -> dict:
    key = jax.random.key(seed)
    ks = jax.random.split(key, 20)
    att_cache = min(ATT_WIN, PAST_LEN)

    def nrm(k, shape, s):
        return jax.random.normal(k, shape, jnp.float32) * s

    return {
        "x_prompt": nrm(ks[0], (BATCH, SEQ, D_MODEL), 1.0),
        "x_sample": nrm(ks[1], (DEC_BATCH, DEC_SEQ, D_MODEL), 1.0),
        "mem_prompt": nrm(ks[2], (BATCH, N_MEM, D_MODEL), 1.0),
        "cache_conv": nrm(ks[3], (DEPTH, DEC_BATCH, CONV_W - 1, D_CONV), 0.5),
        "cache_att_k": nrm(ks[4], (DEPTH, DEC_BATCH, att_cache, N_ATT_HEADS, HEAD_DIM), 1.0),
        "cache_att_v": nrm(ks[5], (DEPTH, DEC_BATCH, att_cache, N_ATT_HEADS, HEAD_DIM), 1.0),
        "cache_mem_k": nrm(ks[6], (DEPTH, DEC_BATCH, N_MEM, N_MEM_HEADS, HEAD_DIM), 1.0),
        "cache_mem_v": nrm(ks[7], (DEPTH, DEC_BATCH, N_MEM, N_MEM_HEADS, HEAD_DIM), 1.0),
        "w_in": nrm(ks[8], (DEPTH, D_MODEL, D_IN), D_MODEL ** -0.5),
        "conv_w": nrm(ks[9], (DEPTH, CONV_W, D_CONV), CONV_W ** -0.5),
        "conv_b": nrm(ks[10], (DEPTH, D_CONV), 0.02),
        "conv_ln_g": 1.0 + nrm(ks[11], (DEPTH, D_CONV), 0.05),
        "conv_ln_b": nrm(ks[12], (DEPTH, D_CONV), 0.02),
        "w_pw": nrm(ks[13], (DEPTH, D_CONV, D_CONV), D_CONV ** -0.5),
        "rel_table": nrm(ks[14], (DEPTH, N_ATT_HEADS, 2 * REL_CLIP + 1), 0.1),
        "w_mem_kv": nrm(ks[15], (DEPTH, D_MODEL, 2 * D_MEM), D_MODEL ** -0.5),
        "w_out": nrm(ks[16], (DEPTH, D_MIX, D_MODEL), BETA * D_MIX ** -0.5),
        "ln_g": 1.0 + nrm(ks[17], (DEPTH, D_MODEL), 0.05),
        "ln_b": nrm(ks[18], (DEPTH, D_MODEL), 0.02),
    }


def reference(x_prompt, x_sample, mem_prompt, cache_conv, cache_att_k, cache_att_v, cache_mem_k, cache_mem_v,
              w_in, conv_w, conv_b, conv_ln_g, conv_ln_b, w_pw, rel_table, w_mem_kv, w_out, ln_g, ln_b):
    B, T, _ = x_prompt.shape
    n_mem = mem_prompt.shape[1]
    keep = min(ATT_WIN, T)
    xp, xs = x_prompt, x_sample
    conv_p, kp_l, vp_l, mk_l, mv_l = [], [], [], [], []
    conv_s, ks_l, vs_l = [], [], []
    for l in range(DEPTH):
        mk, mv = jnp.split(mem_prompt @ w_mem_kv[l], 2, axis=-1)
        mk = mk.reshape(B, n_mem, N_MEM_HEADS, HEAD_DIM)
        mv = mv.reshape(B, n_mem, N_MEM_HEADS, HEAD_DIM)
        left = jnp.zeros((B, CONV_W - 1, D_CONV), xp.dtype)
        xp, cs, k, v = layer_step(xp, mk, mv, left, None, None, w_in[l], conv_w[l], conv_b[l], conv_ln_g[l],
                                  conv_ln_b[l], w_pw[l], rel_table[l], w_out[l], ln_g[l], ln_b[l])
        conv_p.append(cs)
        kp_l.append(k[:, T - keep:])
        vp_l.append(v[:, T - keep:])
        mk_l.append(mk)
        mv_l.append(mv)
        xs, cs2, k2, v2 = layer_step(xs, cache_mem_k[l], cache_mem_v[l], cache_conv[l], cache_att_k[l],
                                     cache_att_v[l], w_in[l], conv_w[l], conv_b[l], conv_ln_g[l], conv_ln_b[l],
                                     w_pw[l], rel_table[l], w_out[l], ln_g[l], ln_b[l])
        conv_s.append(cs2)
        ks_l.append(k2)
        vs_l.append(v2)
    return (xp, xs, jnp.stack(conv_p), jnp.stack(kp_l), jnp.stack(vp_l), jnp.stack(mk_l), jnp.stack(mv_l),
            jnp.stack(conv_s), jnp.stack(ks_l), jnp.stack(vs_l))
```

```python
import numpy as np
import concourse.bass as bass
import concourse.mybir as mybir
from concourse.bass_utils import run_bass_kernel_spmd

F32 = mybir.dt.float32
F32R = mybir.dt.float32r
BF16 = mybir.dt.bfloat16
AF = mybir.ActivationFunctionType
ALU = mybir.AluOpType

D = 2048
SEQ = 2048
DEPTH = 4
NCORES = 8
DEC_SEQ = 64
CW = 31
DC = 1024
HD = 128
NH = 4
NMEM = 256
DIN = 6144
ALPHA = float((2 * DEPTH) ** 0.25)
EPS = 1e-5
SCALE = float(HD ** -0.5)
TBP = 512
NBLK = SEQ // TBP
EXP_CLAMP = 60.0

MM_DT = BF16
ATT_DT = BF16
N_LAYERS = DEPTH
STOP_AT = 99
ABL = set()
NW_SLOTS = 6
USE_WCACHE = True
RAW_ONLY_SAME_ENG = False
STAT_BF16 = True
FUSE_WAIT = True

ENGS = ("pe", "act", "dve", "pool", "sp")


class Res:
    __slots__ = ("name", "lw", "rd", "excl")

    def __init__(self, name="", excl=False):
        self.name = name
        self.lw = None
        self.rd = []
        self.excl = excl


class Op:
    __slots__ = ("eng", "fn", "deps", "eidx", "need_inc", "count", "dma", "dsem", "dcount", "waits", "vc", "tag", "gidx")


class Prog:
    def __init__(self, nc, n_dma_sems=8, same_eng_dist=10 ** 9):
        self.nc = nc
        self.ops = []
        self.eng_ops = {e: [] for e in ENGS}
        self.n_dma_sems = n_dma_sems
        self.same_eng_dist = same_eng_dist
        self.tag = ""

    def add(self, eng, fn, reads=(), writes=(), dma=False):
        op = Op()
        op.eng = eng
        op.fn = fn
        op.dma = dma
        op.need_inc = False
        op.count = 0
        op.dsem = None
        op.dcount = 0
        op.waits = []
        op.vc = None
        op.tag = self.tag
        deps = []
        seen = set()
        for r in reads:
            a = r.lw
            if a is not None and id(a) not in seen:
                seen.add(id(a))
                deps.append((a, True))
            if r.excl:
                last = {}
                for a in r.rd:
                    if a.eng != eng:
                        last[a.eng] = a
                for a in last.values():
                    if id(a) not in seen:
                        seen.add(id(a))
                        deps.append((a, False))
        for w in writes:
            a = w.lw
            if a is not None and id(a) not in seen:
                seen.add(id(a))
                deps.append((a, False))
            last = {}
            for a in w.rd:
                if a.dma:
                    if id(a) not in seen:
                        seen.add(id(a))
                        deps.append((a, False))
                else:
                    last[a.eng] = a
            for a in last.values():
                if id(a) not in seen:
                    seen.add(id(a))
                    deps.append((a, False))
        for r in reads:
            r.rd.append(op)
        for w in writes:
            w.lw = op
            w.rd = []
        op.deps = deps
        op.eidx = len(self.eng_ops[eng])
        op.gidx = len(self.ops)
        self.eng_ops[eng].append(op)
        self.ops.append(op)
        return op

    def resolve(self):
        nc = self.nc
        for op in self.ops:
            kept = []
            for a, raw in op.deps:
                if a is op:
                    continue
                if a.dma:
                    kept.append(a)
                    continue
                if a.eng == op.eng:
                    if op.dma:
                        a.need_inc = True
                        kept.append(a)
                        continue
                    if op.eng == "pe":
                        continue
                    if op.eidx - a.eidx <= self.same_eng_dist and (raw or not RAW_ONLY_SAME_ENG):
                        a.need_inc = True
                        kept.append(a)
                    continue
                a.need_inc = True
                kept.append(a)
            op.deps = kept
        self.sems = {e: nc.alloc_semaphore("s_" + e) for e in ENGS}
        self.dma_sems = {}
        dma_state = {}
        for e in ENGS:
            c = 0
            nd = 0
            for op in self.eng_ops[e]:
                if op.dma:
                    k = nd % self.n_dma_sems
                    nd += 1
                    key = (e, k)
                    if key not in self.dma_sems:
                        self.dma_sems[key] = nc.alloc_semaphore("d_%s%d" % (e, k))
                        dma_state[key] = 0
                    dma_state[key] += 16
                    op.dsem = key
                    op.dcount = dma_state[key]
                elif op.need_inc:
                    c += 1
                    op.count = c
        self.final_dma = dict(dma_state)
        know = {e: {} for e in ENGS}
        for op in self.ops:
            kn = know[op.eng]
            waits = []
            if op.dma and op.dcount > 16:
                if kn.get(op.dsem, 0) < op.dcount - 16:
                    waits.append((op.dsem, op.dcount - 16))
                    kn[op.dsem] = op.dcount - 16
            for a in sorted(op.deps, key=lambda d: -d.gidx):
                if a.dma:
                    key, cnt = a.dsem, a.dcount
                else:
                    key, cnt = a.eng, a.count
                if kn.get(key, 0) >= cnt:
                    continue
                waits.append((key, cnt))
                kn[key] = cnt
                if a.vc is not None:
                    for k2, v2 in a.vc.items():
                        if kn.get(k2, 0) < v2:
                            kn[k2] = v2
            op.waits = waits
            if op.dma:
                vc = dict(kn)
                vc[op.dsem] = op.dcount
                op.vc = vc
            elif op.need_inc:
                vc = dict(kn)
                vc[op.eng] = op.count
                op.vc = vc

    def _sem(self, key):
        if isinstance(key, tuple):
            return self.dma_sems[key]
        return self.sems[key]

    def emit(self, block):
        self.resolve()

        def run(engname, eng):
            for op in self.eng_ops[engname]:
                ws = op.waits
                if FUSE_WAIT and ws:
                    for key, cnt in ws[:-1]:
                        eng.wait_ge(self._sem(key), cnt)
                    ins = op.fn(eng)
                    ins._wait_ge(self._sem(ws[-1][0]), ws[-1][1])
                else:
                    for key, cnt in ws:
                        eng.wait_ge(self._sem(key), cnt)
                    ins = op.fn(eng)
                if op.dma:
                    ins.then_inc(self.dma_sems[op.dsem], 16)
                elif op.need_inc:
                    ins.then_inc(self.sems[engname], 1)
            for key, cnt in self.final_dma.items():
                if key[0] == engname:
                    eng.wait_ge(self.dma_sems[key], cnt)

        if self.eng_ops["sp"]:
            block.sync(lambda e: run("sp", e))
        if self.eng_ops["pe"]:
            block.tensor(lambda e: run("pe", e))
        if self.eng_ops["act"]:
            block.scalar(lambda e: run("act", e))
        if self.eng_ops["dve"]:
            block.vector(lambda e: run("dve", e))
        if self.eng_ops["pool"]:
            block.gpsimd(lambda e: run("pool", e))


_DBG = {}


class Ring:
    def __init__(self, items):
        self.items = items
        self.i = 0

    def next(self):
        it = self.items[self.i % len(self.items)]
        self.i += 1
        return it


def build_program():
    nc = bass.Bass("TRN2", target_bir_lowering=False)
    L = N_LAYERS

    def din(name, shape, dt=F32):
        return nc.dram_tensor(name, list(shape), dt, kind="ExternalInput").ap()

    def dout(name, shape):
        return nc.dram_tensor(name, list(shape), F32, kind="ExternalOutput").ap()

    xp_d = din("xp", [SEQ, D])
    xs_d = din("xs", [2 * DEC_SEQ, D])
    memp_d = din("memp", [NMEM, D])
    cconv_d = din("cconv", [DEPTH, 2, CW - 1, DC])
    catk_d = din("catk", [DEPTH, 2, 512, 512])
    catv_d = din("catv", [DEPTH, 2, 512, 512])
    cmk_d = din("cmk", [DEPTH, 2, NMEM, 512])
    cmv_d = din("cmv", [DEPTH, 2, NMEM, 512])
    w_in_d = din("w_in_t", [DEPTH, 48, 128, 16, 128])
    w_kv_d = din("w_kv_t", [DEPTH, 2, 4, 128, 4, 512])
    w_pw_d = din("w_pw_t", [DEPTH, 8, 128, 8, 128])
    w_out_d = din("w_out_t", [DEPTH, 16, 128, 16, 128])
    w_mkv_d = din("w_mkv_t", [DEPTH, 2, 4, 128, 4, 512])
    convd_d = din("convd_t", [DEPTH, 8, 128, CW, 128])
    convw_d = din("convw_t", [128, DEPTH, 8, CW])
    cpar_d = din("cpar_t", [128, DEPTH, 3, 8])
    lpar_d = din("lpar_t", [128, DEPTH, 2, 16])
    bias_d = din("bias_t", [DEPTH, 128, NH, 5, 128])
    ident_d = din("ident", [128, 128])
    yp_d = dout("y_p", [SEQ, D])
    ys_d = dout("y_s", [2 * DEC_SEQ, D])
    ncp_d = dout("ncp", [DEPTH, CW - 1, DC])
    nkp_d = dout("nkp", [DEPTH, 512, 512])
    nvp_d = dout("nvp", [DEPTH, 512, 512])
    nmk_d = dout("nmk", [DEPTH, NMEM, 512])
    nmv_d = dout("nmv", [DEPTH, NMEM, 512])
    ncs_d = dout("ncs", [DEPTH, 2, CW - 1, DC])
    nks_d = dout("nks", [DEPTH, 2, DEC_SEQ, 512])
    nvs_d = dout("nvs", [DEPTH, 2, DEC_SEQ, 512])

    def dscratch(name, shape):
        return nc.dram_tensor(name, list(shape), MM_DT, kind="Internal").ap()

    wc_in = dscratch("wc_in", [DEPTH, 48, 128, 2048])
    wc_kv = dscratch("wc_kv", [DEPTH, 2, 4, 128, 2048])
    wc_pw = dscratch("wc_pw", [DEPTH, 8, 128, 1024])
    wc_out = dscratch("wc_out", [DEPTH, 16, 128, 2048])
    wc_cv = dscratch("wc_cv", [DEPTH, 8, 2, 128, 2048])
    wcache = {}

    P = Prog(nc)
    _DBG["P"] = P
    sb = nc.alloc_sbuf_tensor

    x32 = sb("x32", [128, 16, TBP], F32)
    r_x32 = [Res("x32_%d" % c) for c in range(16)]
    xb = sb("xb", [128, 16, TBP], MM_DT)
    r_xb = [Res("xb%d" % c) for c in range(16)]
    ybuf = sb("ybuf", [128, 8, TBP], F32)
    mix = ybuf[:, :, :].bitcast(MM_DT).rearrange("p a (b c) -> p (a b) c", b=2)
    r_mix = [Res("mix%d" % c) for c in range(16)]

    def r_y(j):
        return [r_mix[2 * j], r_mix[2 * j + 1]]

    sact = sb("s_act", [128, 8, TBP], MM_DT)
    r_s = [Res("s%d" % c) for c in range(8)]
    UW = CW - 1 + TBP
    ub = sb("ub", [128, 8, UW], MM_DT)
    r_u = [Res("u%d" % c) for c in range(8)]
    ctail = [sb("ctail%d" % l, [128, 8, CW - 1], F32) for l in range(L)]
    r_ctail = [Res("ctail%d" % l) for l in range(L)]
    tail_s = sb("tail_s", [128, 8, 2 * (CW - 1)], F32)
    r_tail_s = Res("tail_s")
    NKB = max(L + 1, 3)
    kbuf = [sb("kbuf%d" % i, [128, NH, 512], ATT_DT) for i in range(NKB)]
    r_kbuf = [[Res("k%d_%d" % (i, h)) for h in range(NH)] for i in range(NKB)]
    vbuf = [sb("vbuf%d" % i, [128, 4, 512], ATT_DT) for i in range(NKB)]
    r_vbuf = [[Res("v%d_%d" % (i, t)) for t in range(4)] for i in range(NKB)]
    qt = sb("qt", [128, NH, TBP], ATT_DT)
    r_q = [Res("q%d" % h) for h in range(NH)]
    mkt = [sb("mkt%d" % l, [128, NH, NMEM], ATT_DT) for l in range(L)]
    r_mkt = [Res("mkt%d" % l) for l in range(L)]
    mv = [sb("mv%d" % l, [128, 2, 512], ATT_DT) for l in range(L)]
    r_mv = [Res("mv%d" % l) for l in range(L)]
    NW = NW_SLOTS
    wring = Ring([(sb("wr%d" % i, [128, 2048], MM_DT), Res("wr%d" % i)) for i in range(NW)])
    tmpA = Ring([(sb("tA%d" % i, [128, TBP], F32), Res("tA%d" % i)) for i in range(3)])
    ptr = Ring([(sb("pt%d" % i, [128, 640], ATT_DT), Res("pt%d" % i)) for i in range(4)])
    stat = [sb("st%d" % i, [128, TBP], F32) for i in range(3)]
    r_stat = [Res("st%d" % i) for i in range(3)]
    stg = Ring([(sb("stg%d" % i, [128, 1024], F32), Res("stg%d" % i)) for i in range(2)])
    bias_sb = sb("bias_sb", [128, NH, 5, 128], ATT_DT)
    r_bias = Res("bias")
    convw = sb("convw", [128, DEPTH, 8, CW], F32)
    cpar = sb("cpar", [128, DEPTH, 3, 8], F32)
    lpar = sb("lpar", [128, DEPTH, 2, 16], F32)
    ident = sb("ident_sb", [128, 128], F32)
    ones = sb("ones_sb", [128, 128], F32)
    ones_a = sb("ones_a", [128, 128], ATT_DT)
    ident_a = sb("ident_a", [128, 128], ATT_DT)
    epsc = sb("epsc", [128, 1], F32)
    r_const = Res("const")
    ps = [nc.alloc_psum_tensor("ps%d" % i, [128, 512], F32) for i in range(8)]
    r_ps = [Res("ps%d" % i, excl=True) for i in range(8)]

    def dma(q, out, in_, reads=(), writes=()):
        P.add(q, lambda e: e.dma_start(out=out, in_=in_), reads=reads, writes=writes, dma=True)

    def mm(out, lhsT, rhs, start, stop, reads, writes):
        P.add("pe", lambda e: e.matmul(out, lhsT=lhsT, rhs=rhs, start=start, stop=stop), reads=reads, writes=writes)

    def tr(out, in_, k, reads, writes):
        P.add("pe", lambda e: e.transpose(out=out, in_=in_, identity=ident[0:k, 0:k]), reads=list(reads) + [r_const],
              writes=writes)

    def act(out, in_, func, reads, writes, scale=1.0, bias=0.0):
        P.add("act", lambda e: e.activation(out=out, in_=in_, func=func, bias=bias, scale=scale), reads=reads,
              writes=writes)

    def tt(out, in0, in1, op, reads, writes, eng="dve"):
        P.add(eng, lambda e: e.tensor_tensor(out=out, in0=in0, in1=in1, op=op), reads=reads, writes=writes)

    def stt(out, in0, scalar, in1, op0, op1, reads, writes, eng="dve"):
        P.add(eng, lambda e: e.scalar_tensor_tensor(out=out, in0=in0, scalar=scalar, in1=in1, op0=op0, op1=op1),
              reads=reads, writes=writes)

    def ts(out, in0, s1, s2, op0, op1, reads, writes, eng="dve"):
        if s2 is None:
            P.add(eng, lambda e: e.tensor_scalar(out=out, in0=in0, scalar1=s1, scalar2=None, op0=op0), reads=reads,
                  writes=writes)
        else:
            P.add(eng, lambda e: e.tensor_scalar(out=out, in0=in0, scalar1=s1, scalar2=s2, op0=op0, op1=op1),
                  reads=reads, writes=writes)

    def cp(out, in_, reads, writes, eng="dve"):
        P.add(eng, lambda e: e.tensor_copy(out=out, in_=in_), reads=reads, writes=writes)

    def recip(out, in_, reads, writes):
        P.add("dve", lambda e: e.reciprocal(out=out, in_=in_), reads=reads, writes=writes)

    def load_w(src_ap, shape3, cache=None):
        t, r = wring.next()
        a, b_ = shape3
        n = a * b_
        view = t[:, 0:n].rearrange("p (a b) -> p a b", a=a)
        if cache is None or not USE_WCACHE:
            dma("pool", view, src_ap, writes=[r])
            return view, r
        key, cap = cache
        if key not in wcache:
            rc = Res("wc")
            wcache[key] = rc
            dma("pool", view, src_ap, writes=[r])
            dma("sp", cap, t[:, 0:n], reads=[r], writes=[rc])
        else:
            dma("pool", t[:, 0:n], cap, reads=[wcache[key]], writes=[r])
        return view, r

    def bank4(bk, n):
        return ps[bk][:, :].rearrange("p (a b) -> p a b", a=4)[:, :, 0:n]

    dma("sp", ident[:], ident_d, writes=[r_const])
    dma("sp", convw[:], convw_d, writes=[r_const])
    dma("sp", cpar[:], cpar_d, writes=[r_const])
    dma("sp", lpar[:], lpar_d, writes=[r_const])
    P.add("dve", lambda e: e.memset(ones[:], 1.0), writes=[r_const])
    P.add("dve", lambda e: e.memset(ones_a[:], 1.0), writes=[r_const])
    cp(ident_a[:], ident[:], [r_const], [r_const])
    P.add("dve", lambda e: e.memset(epsc[:], EPS), writes=[r_const])

    def load_tokens(src_rows_ap, ntok, want32):
        for t0 in range(0, ntok, 128):
            n = min(128, ntok - t0)
            for hf in range(2):
                st, r_st = stg.next()
                dma("sp", st[0:n, :], src_rows_ap[t0:t0 + n, hf * 1024:(hf + 1) * 1024], writes=[r_st])
                for g in range(2):
                    bk = 6 + g
                    c0 = hf * 8 + g * 4
                    for cc in range(4):
                        tr(ps[bk][:, cc * 128:cc * 128 + n], st[0:n, (g * 4 + cc) * 128:(g * 4 + cc + 1) * 128], n,
                           [r_st], [r_ps[bk]])
                    src = bank4(bk, n)
                    act(xb[:, c0:c0 + 4, t0:t0 + n], src, AF.Identity, [r_ps[bk]], r_xb[c0:c0 + 4])
                    if want32:
                        cp(x32[:, c0:c0 + 4, t0:t0 + n], src, [r_ps[bk]], r_x32[c0:c0 + 4])

    def tokmajor_proj(groups, w_src, evac, ckey=None, csrc=None):
        assert len(groups) <= 4
        for pc in range(4):
            wv, r_w = load_w(w_src[pc], (4, 512), None if ckey is None else (ckey + (pc,), csrc[pc]))
            for i, (c0, ncol) in enumerate(groups):
                bk = 4 + i
                for cc in range(4):
                    c = pc * 4 + cc
                    mm(ps[bk][0:ncol, :], xb[:, c, c0:c0 + ncol], wv[:, cc, :], c == 0, c == 15,
                       [r_w, r_xb[c]], [r_ps[bk]])
        for i, (c0, ncol) in enumerate(groups):
            evac(i, ps[4 + i][0:ncol, :], r_ps[4 + i])

    def mem_kv_from_staging(l, m, st, r_st):
        cp(mv[l][:, m, :], st[:, 512:1024], [r_st], [r_mv[l]])
        bk = 6 + (m % 2)
        for h in range(NH):
            tr(ps[bk][:, h * 128:(h + 1) * 128], st[:, h * 128:(h + 1) * 128], 128, [r_st], [r_ps[bk]])
        act(mkt[l][:, :, m * 128:(m + 1) * 128], bank4(bk, 128), AF.Identity, [r_ps[bk]], [r_mkt[l]])

    load_tokens(memp_d, NMEM, False)
    for l in range(L):
        stk = [None, None]

        def ev_k(i, pap, rp, l=l, stk=stk):
            st, r_st = stg.next()
            cp(st[:, 0:512], pap, [rp], [r_st])
            dma("sp", nmk_d[l, i * 128:(i + 1) * 128, :], st[:, 0:512], reads=[r_st])
            stk[i] = (st, r_st)

        tokmajor_proj([(0, 128), (128, 128)], w_mkv_d[l, 0], ev_k)

        def ev_v(i, pap, rp, l=l, stk=stk):
            st, r_st = stk[i]
            act(st[:, 512:1024], pap, AF.Identity, [rp], [r_st])
            dma("sp", nmv_d[l, i * 128:(i + 1) * 128, :], st[:, 512:1024], reads=[r_st])

        tokmajor_proj([(0, 128), (128, 128)], w_mkv_d[l, 1], ev_v)
        for m in range(2):
            st, r_st = stk[m]
            mem_kv_from_staging(l, m, st, r_st)

    kfree = list(range(NKB))
    kstate = [None] * L

    def layer_block(l, TB, segs, blk, is_sample):
        NSEG = len(segs)
        SL = segs[0][1]
        SEGW = CW - 1 + SL
        NT = (TB + 127) // 128
        last_prompt = (not is_sample) and blk == NBLK - 1
        want_kv_out = is_sample or last_prompt
        uview = ub[:, :, 0:NSEG * SEGW].rearrange("p c (s w) -> p c s w", s=NSEG)

        def inproj(j, bk):
            wv, r_w = load_w(w_in_d[l, j], (16, 128), (("in", l, j), wc_in[l, j]))
            for c in range(16):
                mm(ps[bk][:, 0:TB], wv[:, c, :], xb[:, c, 0:TB], c == 0, c == 15, [r_w, r_xb[c]], [r_ps[bk]])

        def stat_tile(ap_, rr, first, last, b1, b2):
            tA, r_tA = tmpA.next()
            tb_ = tA[:, :].bitcast(ATT_DT)
            sq = tb_[:, 0:TB]
            zb = tb_[:, 512:512 + TB]
            act(sq, ap_, AF.Square, rr, [r_tA])
            if STAT_BF16:
                act(zb, ap_, AF.Identity, rr, [r_tA])
                mm(ps[b1][:, 0:TB], ones_a[:, :], zb, first, last, [r_const, r_tA], [r_ps[b1]])
            else:
                mm(ps[b1][:, 0:TB], ones[:, :], ap_, first, last, [r_const] + rr, [r_ps[b1]])
            mm(ps[b2][:, 0:TB], ones_a[:, :], sq, first, last, [r_const, r_tA], [r_ps[b2]])

        def stat_finish(nfeat, b1, b2):
            inv = 1.0 / nfeat
            act(stat[1][:, 0:TB], ps[b1][:, 0:TB], AF.Square, [r_ps[b1]], [r_stat[1]], scale=inv)
            ts(stat[0][:, 0:TB], ps[b1][:, 0:TB], inv, None, ALU.mult, None, [r_ps[b1]], [r_stat[0]])
            stt(stat[1][:, 0:TB], ps[b2][:, 0:TB], inv, stat[1][:, 0:TB], ALU.mult, ALU.subtract,
                [r_ps[b2], r_stat[1]], [r_stat[1]])
            act(stat[2][:, 0:TB], stat[1][:, 0:TB], AF.Sqrt, [r_stat[1], r_const], [r_stat[2]],
                bias=epsc[:, 0:1])
            recip(stat[2][:, 0:TB], stat[2][:, 0:TB], [r_stat[2]], [r_stat[2]])
            stt(stat[0][:, 0:TB], stat[0][:, 0:TB], -1.0, stat[2][:, 0:TB], ALU.mult, ALU.mult,
                [r_stat[0], r_stat[2]], [r_stat[0]])

        dma("pool", bias_sb[:], bias_d[l], writes=[r_bias])

        P.tag = 'S1_glu_conv'
        for s in range(NSEG):
            if is_sample:
                st, r_st = stg.next()
                dma("sp", st[0:CW - 1, :], cconv_d[l, s], writes=[r_st])
                for g in range(2):
                    bk = 6 + g
                    for jj in range(4):
                        j = g * 4 + jj
                        tr(ps[bk][:, jj * (CW - 1):(jj + 1) * (CW - 1)], st[0:CW - 1, j * 128:(j + 1) * 128],
                           CW - 1, [r_st], [r_ps[bk]])
                    cp(uview[:, g * 4:(g + 1) * 4, s, 0:CW - 1],
                       ps[bk][:, 0:4 * (CW - 1)].rearrange("p (a b) -> p a b", a=4), [r_ps[bk]],
                       r_u[g * 4:(g + 1) * 4])
            elif blk == 0:
                P.add("dve", lambda e: e.memset(uview[:, :, 0, 0:CW - 1], 0.0), writes=r_u)
            else:
                cp(uview[:, :, 0, 0:CW - 1], ctail[l][:, :, :], [r_ctail[l]], r_u)
        tailbuf, r_tail = (tail_s, r_tail_s) if is_sample else (ctail[l], r_ctail[l])

        def glu(j):
            ba, bb = 2 * (j % 2), 2 * (j % 2) + 1
            inproj(j, ba)
            inproj(8 + j, bb)
            tA, r_tA = tmpA.next()
            act(tA[:, 0:TB], ps[bb][:, 0:TB], AF.Sigmoid, [r_ps[bb]], [r_tA])
            for s in range(NSEG):
                c0 = segs[s][0]
                tt(uview[:, j, s, CW - 1:SEGW], ps[ba][:, c0:c0 + SL], tA[:, c0:c0 + SL], ALU.mult,
                   [r_ps[ba], r_tA], [r_u[j]])
                tt(tailbuf[:, j, s * (CW - 1):(s + 1) * (CW - 1)], ps[ba][:, c0 + SL - (CW - 1):c0 + SL],
                   tA[:, c0 + SL - (CW - 1):c0 + SL], ALU.mult, [r_ps[ba], r_tA], [r_tail])

        def conv(j):
            bk = 4 + (j % 2)
            wa, r_wa = load_w(convd_d[l, j, :, 0:16, :], (16, 128), (("cv", l, j, 0), wc_cv[l, j, 0]))
            wb, r_wb = load_w(convd_d[l, j, :, 16:CW, :], (CW - 16, 128),
                              (("cv", l, j, 1), wc_cv[l, j, 1][:, 0:(CW - 16) * 128]))
            if NSEG == 1:
                for k in range(CW):
                    wv, r_w = (wa, r_wa) if k < 16 else (wb, r_wb)
                    mm(ps[bk][:, 0:SL], wv[:, k % 16, :], uview[:, j, 0, k:k + SL], k == 0, k == CW - 1,
                       [r_w, r_u[j]], [r_ps[bk]])
            else:
                o3 = ps[bk][:, 0:TB].rearrange("p (s q) -> p s q", s=NSEG)
                for k in range(CW):
                    wv, r_w = (wa, r_wa) if k < 16 else (wb, r_wb)
                    mm(o3, wv[:, k % 16, :], uview[:, j, :, k:k + SL], k == 0, k == CW - 1,
                       [r_w, r_u[j]], [r_ps[bk]])
            act(ybuf[:, j, 0:TB], ps[bk][:, 0:TB], AF.Identity, [r_ps[bk], r_const], r_y(j),
                bias=cpar[:, l, 0, j:j + 1])

        def glu_evac(j):
            ba, bb = 2 * (j % 2), 2 * (j % 2) + 1
            tA, r_tA = tmpA.next()
            act(tA[:, 0:TB], ps[bb][:, 0:TB], AF.Sigmoid, [r_ps[bb]], [r_tA])
            for s in range(NSEG):
                c0 = segs[s][0]
                tt(uview[:, j, s, CW - 1:SEGW], ps[ba][:, c0:c0 + SL], tA[:, c0:c0 + SL], ALU.mult,
                   [r_ps[ba], r_tA], [r_u[j]])
                tt(tailbuf[:, j, s * (CW - 1):(s + 1) * (CW - 1)], ps[ba][:, c0 + SL - (CW - 1):c0 + SL],
                   tA[:, c0 + SL - (CW - 1):c0 + SL], ALU.mult, [r_ps[ba], r_tA], [r_tail])

        wl = [load_w(w_in_d[l, jj], (16, 128), (("in", l, jj), wc_in[l, jj])) for jj in (0, 8, 1, 9)]
        for c in range(16):
            for t_, (wv, r_w) in enumerate(wl):
                mm(ps[t_][:, 0:TB], wv[:, c, :], xb[:, c, 0:TB], c == 0, c == 15, [r_w, r_xb[c]], [r_ps[t_]])
        glu_evac(0)
        glu_evac(1)
        for j in range(8):
            if 1 <= j and j + 1 < 8:
                glu(j + 1)
            conv(j)
            if j >= 1:
                stat_tile(ybuf[:, j - 1, 0:TB], r_y(j - 1), j == 1, False, 6, 7)
        stat_tile(ybuf[:, 7, 0:TB], r_y(7), False, True, 6, 7)
        if want_kv_out:
            for s in range(NSEG):
                st, r_st = stg.next()
                for g in range(2):
                    bk = 0 + g
                    for jj in range(4):
                        j = g * 4 + jj
                        tr(ps[bk][0:CW - 1, jj * 128:(jj + 1) * 128],
                           tailbuf[:, j, s * (CW - 1):(s + 1) * (CW - 1)], 128, [r_tail], [r_ps[bk]])
                    cp(st[0:CW - 1, g * 512:(g + 1) * 512], ps[bk][0:CW - 1, :], [r_ps[bk]], [r_st])
                dst = ncs_d[l, s] if is_sample else ncp_d[l]
                dma("sp", dst, st[0:CW - 1, :], reads=[r_st])

        if STOP_AT <= 1:
            return
        P.tag = 'S3_convln'
        stat_finish(DC, 6, 7)
        LAG = 2
        for jj in range(8 + 2 * LAG):
            j = jj
            if j < 8:
                tt(ybuf[:, j, 0:TB], ybuf[:, j, 0:TB], stat[2][:, 0:TB], ALU.mult, r_y(j) + [r_stat[2]], r_y(j))
            j = jj - LAG
            if 0 <= j < 8:
                tt(ybuf[:, j, 0:TB], ybuf[:, j, 0:TB], stat[0][:, 0:TB], ALU.add, r_y(j) + [r_stat[0]], r_y(j))
            j = jj - 2 * LAG
            if 0 <= j < 8:
                act(sact[:, j, 0:TB], ybuf[:, j, 0:TB], AF.Silu, r_y(j) + [r_const], [r_s[j]],
                    scale=cpar[:, l, 1, j:j + 1], bias=cpar[:, l, 2, j:j + 1])

        if STOP_AT <= 3:
            return
        P.tag = 'S5_qkv'
        kc = kfree.pop(0)
        for h in range(NH):
            bk = h % 4
            inproj(24 + h, bk)
            act(qt[:, h, 0:TB], ps[bk][:, 0:TB], AF.Identity, [r_ps[bk]], [r_q[h]], scale=SCALE)
        for h in range(NH):
            bk = h % 4
            inproj(28 + h, bk)
            cp(kbuf[kc][:, h, 0:TB], ps[bk][:, 0:TB], [r_ps[bk]], [r_kbuf[kc][h]])

        if is_sample:
            groups = [(segs[s][0], SL) for s in range(NSEG)]
        else:
            groups = [(t * 128, 128) for t in range(NT)]

        def ev_v(i, pap, rp):
            n = groups[i][1]
            cp(vbuf[kc][0:n, i, :], pap, [rp], [r_vbuf[kc][i]])
            if want_kv_out:
                st, r_st = stg.next()
                act(st[0:n, 0:512], pap, AF.Identity, [rp], [r_st])
                dst = nvs_d[l, i] if is_sample else nvp_d[l, i * 128:(i + 1) * 128, :]
                dma("sp", dst, st[0:n, 0:512], reads=[r_st])

        tokmajor_proj(groups, w_kv_d[l, 1], ev_v, ("kv", l, 1), wc_kv[l, 1])
        if want_kv_out:
            def ev_k(i, pap, rp):
                n = groups[i][1]
                st, r_st = stg.next()
                act(st[0:n, 0:512], pap, AF.Identity, [rp], [r_st])
                dst = nks_d[l, i] if is_sample else nkp_d[l, i * 128:(i + 1) * 128, :]
                dma("sp", dst, st[0:n, 0:512], reads=[r_st])

            tokmajor_proj(groups, w_kv_d[l, 0], ev_k, ("kv", l, 0), wc_kv[l, 0])

        if STOP_AT <= 2:
            return
        P.tag = 'S4_pw'
        for j in range(8):
            bp, bg = 2 * (j % 2), 2 * (j % 2) + 1
            wv, r_w = load_w(w_pw_d[l, j], (8, 128), (("pw", l, j), wc_pw[l, j]))
            for c in range(8):
                mm(ps[bp][:, 0:TB], wv[:, c, :], sact[:, c, 0:TB], c == 0, c == 7, [r_w, r_s[c]], [r_ps[bp]])
            inproj(16 + j, bg)
            tA, r_tA = tmpA.next()
            act(tA[:, 0:TB], ps[bg][:, 0:TB], AF.Silu, [r_ps[bg]], [r_tA])
            tt(mix[:, j, 0:TB], ps[bp][:, 0:TB], tA[:, 0:TB], ALU.mult, [r_ps[bp], r_tA], [r_mix[j]])

        if STOP_AT <= 4:
            return
        P.tag = 'S6_att'
        qgroups = []
        kcache = []
        if is_sample:
            for s in range(NSEG):
                kb = kfree.pop(0)
                kcache.append(kb)
                for t in range(4):
                    st, r_st = stg.next()
                    dma("sp", st[:, 0:512], catk_d[l, s, t * 128:(t + 1) * 128, :], writes=[r_st])
                    dma("sp", st[:, 512:1024], catv_d[l, s, t * 128:(t + 1) * 128, :], writes=[r_st])
                    bk = 6 + (t % 2)
                    for h in range(NH):
                        tr(ps[bk][:, h * 128:(h + 1) * 128], st[:, h * 128:(h + 1) * 128], 128, [r_st], [r_ps[bk]])
                    act(kbuf[kb][:, :, t * 128:(t + 1) * 128], bank4(bk, 128), AF.Identity, [r_ps[bk]], r_kbuf[kb])
                    cp(vbuf[kb][:, t, :], st[:, 512:1024], [r_st], [r_vbuf[kb][t]])
                tiles = [(kb, r * 128, 128, r, kb, r) for r in range(4)]
                tiles.append((kc, segs[s][0], SL, 4, kc, s))
                qgroups.append((segs[s][0], SL, tiles))
        else:
            kp = kstate[l]
            for j in range(NT):
                tiles = []
                for r in range(5):
                    t = j + r - 4
                    if blk * 4 + t < 0:
                        continue
                    if t < 0:
                        tiles.append((kp, (t + 4) * 128, 128, r, kp, t + 4))
                    else:
                        tiles.append((kc, t * 128, 128, r, kc, t))
                qgroups.append((j * 128, 128, tiles))

        units = [(gi, q0, nq, tiles, h) for gi, (q0, nq, tiles) in enumerate(qgroups) for h in range(NH)]
        pts = {}

        def banks_sc(u):
            return ((0, 1), (2, 3), (6, 7))[u % 3]

        def scores(u):
            gi, q0, nq, tiles, h = units[u]
            b0, b1 = banks_sc(u)
            for (kb, kcol, nk, r, vb, vt) in tiles:
                if r < 4:
                    o_ap, r_o = ps[b0][0:nk, r * 128:r * 128 + nq], r_ps[b0]
                else:
                    o_ap, r_o = ps[b1][0:nk, 0:nq], r_ps[b1]
                mm(o_ap, kbuf[kb][:, h, kcol:kcol + nk], qt[:, h, q0:q0 + nq], True, False,
                   [r_kbuf[kb][h], r_q[h]], [r_o])
                mm(o_ap, ident_a[0:nk, 0:nk], bias_sb[0:nk, h, r, 0:nq], False, True, [r_const, r_bias], [r_o])

        def softmax(u):
            gi, q0, nq, tiles, h = units[u]
            b0, b1 = banks_sc(u)
            pt, r_pt = ptr.next()
            pts[u] = (pt, r_pt)
            rs = [t[3] for t in tiles if t[3] < 4]
            if rs:
                rmin = min(rs)
                pv_ = ps[b0][:, :].rearrange("p (r q) -> p r q", q=128)[:, rmin:4, 0:nq]
                act(pt[:, 0:512].rearrange("p (r q) -> p r q", q=128)[:, rmin:4, 0:nq], pv_, AF.Exp, [r_ps[b0]], [r_pt])
            last = [t for t in tiles if t[3] == 4]
            if last:
                nk = last[0][2]
                act(pt[0:nk, 512:512 + nq], ps[b1][0:nk, 0:nq], AF.Exp, [r_ps[b1]], [r_pt])

        def pv(u):
            gi, q0, nq, tiles, h = units[u]
            bo, bl = 4, 5
            pt, r_pt = pts.pop(u)
            nt_ = len(tiles)
            for ti, (kb, kcol, nk, r, vb, vt) in enumerate(tiles):
                mm(ps[bo][:, h * 128:h * 128 + nq], vbuf[vb][0:nk, vt, h * 128:(h + 1) * 128],
                   pt[0:nk, r * 128:r * 128 + nq], ti == 0, ti == nt_ - 1, [r_vbuf[vb][vt], r_pt], [r_ps[bo]])
            for ti, (kb, kcol, nk, r, vb, vt) in enumerate(tiles):
                mm(ps[bl][:, h * 128:h * 128 + nq], ones_a[0:nk, :], pt[0:nk, r * 128:r * 128 + nq], ti == 0,
                   ti == nt_ - 1, [r_const, r_pt], [r_ps[bl]])
            if h == NH - 1:
                tA, r_tA = tmpA.next()
                t4 = tA[:, :].rearrange("p (a b) -> p a b", a=4)[:, :, 0:nq]
                recip(t4, bank4(bl, nq), [r_ps[bl]], [r_tA])
                tt(mix[:, 8:12, q0:q0 + nq], bank4(bo, nq), t4, ALU.mult, [r_ps[bo], r_tA], r_mix[8:12])

        nu = len(units)
        for u in range(min(2, nu)):
            scores(u)
        for u in range(nu):
            if u + 2 < nu:
                scores(u + 2)
            softmax(u)
            pv(u)
        P.tag = 'S6b_attgate'
        for h in range(NH):
            bk = h % 4
            inproj(36 + h, bk)
            tA, r_tA = tmpA.next()
            act(tA[:, 0:TB], ps[bk][:, 0:TB], AF.Silu, [r_ps[bk]], [r_tA])
            tt(mix[:, 8 + h, 0:TB], mix[:, 8 + h, 0:TB], tA[:, 0:TB], ALU.mult, [r_mix[8 + h], r_tA], [r_mix[8 + h]])

        if STOP_AT <= 5:
            return
        P.tag = 'S7_mem'
        for h in range(NH):
            bk = h % 4
            inproj(40 + h, bk)
            act(qt[:, h, 0:TB], ps[bk][:, 0:TB], AF.Identity, [r_ps[bk]], [r_q[h]], scale=SCALE)
        if is_sample:
            for s in range(NSEG):
                for m in range(2):
                    st, r_st = stg.next()
                    dma("sp", st[:, 0:512], cmk_d[l, s, m * 128:(m + 1) * 128, :], writes=[r_st])
                    dma("sp", st[:, 512:1024], cmv_d[l, s, m * 128:(m + 1) * 128, :], writes=[r_st])
                    mem_kv_from_staging(l, m, st, r_st)
                mem_attention(l, segs[s][0], SL)
        else:
            mem_attention(l, 0, TB)
        for h in range(NH):
            bk = h % 4
            inproj(44 + h, bk)
            tA, r_tA = tmpA.next()
            act(tA[:, 0:TB], ps[bk][:, 0:TB], AF.Silu, [r_ps[bk]], [r_tA])
            tt(mix[:, 12 + h, 0:TB], mix[:, 12 + h, 0:TB], tA[:, 0:TB], ALU.mult, [r_mix[12 + h], r_tA],
               [r_mix[12 + h]])

        if STOP_AT <= 6:
            return
        P.tag = 'S8_out'
        for j in range(16):
            bk = j % 4
            wv, r_w = load_w(w_out_d[l, j], (16, 128), (("out", l, j), wc_out[l, j]))
            for c in range(16):
                mm(ps[bk][:, 0:TB], wv[:, c, :], mix[:, c, 0:TB], c == 0, c == 15, [r_w, r_mix[c]], [r_ps[bk]])
            stt(x32[:, j, 0:TB], x32[:, j, 0:TB], ALPHA, ps[bk][:, 0:TB], ALU.mult, ALU.add, [r_x32[j], r_ps[bk]],
                [r_x32[j]])
            if j >= 2:
                stat_tile(x32[:, j - 2, 0:TB], [r_x32[j - 2]], j == 2, False, 4, 5)
        P.tag = 'S8_ln'
        stat_tile(x32[:, 14, 0:TB], [r_x32[14]], False, False, 4, 5)
        stat_tile(x32[:, 15, 0:TB], [r_x32[15]], False, True, 4, 5)
        stat_finish(D, 4, 5)
        final = (l == L - 1)
        LAG = 2
        for jj in range(16 + 2 * LAG):
            j = jj
            if j < 16:
                tt(x32[:, j, 0:TB], x32[:, j, 0:TB], stat[2][:, 0:TB], ALU.mult, [r_x32[j], r_stat[2]], [r_x32[j]])
            j = jj - LAG
            if 0 <= j < 16:
                tt(x32[:, j, 0:TB], x32[:, j, 0:TB], stat[0][:, 0:TB], ALU.add, [r_x32[j], r_stat[0]], [r_x32[j]])
            j = jj - 2 * LAG
            if 0 <= j < 16:
                if not final:
                    act(xb[:, j, 0:TB], x32[:, j, 0:TB], AF.Identity, [r_x32[j], r_const], [r_xb[j]],
                        scale=lpar[:, l, 0, j:j + 1], bias=lpar[:, l, 1, j:j + 1])
                act(x32[:, j, 0:TB], x32[:, j, 0:TB], AF.Identity, [r_x32[j], r_const], [r_x32[j]],
                    scale=lpar[:, l, 0, j:j + 1], bias=lpar[:, l, 1, j:j + 1])
        if final:
            for t0 in range(0, TB, 128):
                for hf in range(2):
                    st, r_st = stg.next()
                    for g in range(2):
                        bk = 6 + g
                        for cc in range(4):
                            c = hf * 8 + g * 4 + cc
                            tr(ps[bk][:, cc * 128:(cc + 1) * 128], x32[:, c, t0:t0 + 128], 128, [r_x32[c]],
                               [r_ps[bk]])
                        if g == 0:
                            cp(st[:, 0:512], ps[bk][:, :], [r_ps[bk]], [r_st])
                        else:
                            act(st[:, 512:1024], ps[bk][:, :], AF.Identity, [r_ps[bk]], [r_st])
                    if is_sample:
                        dma("sp", ys_d[t0:t0 + 128, hf * 1024:(hf + 1) * 1024], st[:, :], reads=[r_st])
                    else:
                        r0 = blk * TBP + t0
                        dma("sp", yp_d[r0:r0 + 128, hf * 1024:(hf + 1) * 1024], st[:, :], reads=[r_st])

        if is_sample:
            kfree.extend(kcache)
            kfree.append(kc)
        else:
            if kstate[l] is not None:
                kfree.append(kstate[l])
            kstate[l] = kc

    def mem_attention(l, q0, nq):
        pts = {}

        def sc(h):
            for m in range(2):
                bk = 2 * (h % 2) + m
                mm(ps[bk][:, 0:nq], mkt[l][:, h, m * 128:(m + 1) * 128], qt[:, h, q0:q0 + nq], True, True,
                   [r_mkt[l], r_q[h]], [r_ps[bk]])

        def sm(h):
            lst = []
            for m in range(2):
                bk = 2 * (h % 2) + m
                pt, r_pt = ptr.next()
                act(pt[:, 0:nq], ps[bk][:, 0:nq], AF.Exp, [r_ps[bk]], [r_pt])
                lst.append((pt, r_pt))
            pts[h] = lst

        def pv(h):
            bo, bl = (4, 5) if h % 2 == 0 else (6, 7)
            lst = pts.pop(h)
            for m in range(2):
                pt, r_pt = lst[m]
                mm(ps[bo][:, 0:nq], mv[l][:, m, h * 128:(h + 1) * 128], pt[:, 0:nq], m == 0, m == 1,
                   [r_mv[l], r_pt], [r_ps[bo]])
            for m in range(2):
                pt, r_pt = lst[m]
                mm(ps[bl][:, 0:nq], ones_a[:, :], pt[:, 0:nq], m == 0, m == 1, [r_const, r_pt], [r_ps[bl]])
            tA, r_tA = tmpA.next()
            recip(tA[:, 0:nq], ps[bl][:, 0:nq], [r_ps[bl]], [r_tA])
            tt(mix[:, 12 + h, q0:q0 + nq], ps[bo][:, 0:nq], tA[:, 0:nq], ALU.mult, [r_ps[bo], r_tA], [r_mix[12 + h]])

        sc(0)
        for h in range(NH):
            if h + 1 < NH:
                sc(h + 1)
            sm(h)
            pv(h)

    for blk in range(NBLK):
        if STOP_AT <= 0 or (STOP_AT <= 7 and blk > 0):
            break
        P.tag = 'load'
        load_tokens(xp_d[blk * TBP:(blk + 1) * TBP, :], TBP, True)
        for l in range(L):
            layer_block(l, TBP, [(0, TBP)], blk, False)
    for l in range(L):
        if kstate[l] is not None:
            kfree.append(kstate[l])
            kstate[l] = None
    if STOP_AT > 8 and 'nosample' not in ABL:
        load_tokens(xs_d, 2 * DEC_SEQ, True)
        for l in range(L):
            layer_block(l, 2 * DEC_SEQ, [(0, DEC_SEQ), (DEC_SEQ, DEC_SEQ)], None, True)

    with nc.Block() as block:
        P.emit(block)
    return nc


def _prep_weights(w_in, conv_w, conv_b, conv_ln_g, conv_ln_b, w_pw, rel_table, w_mem_kv, w_out, ln_g, ln_b):
    f = np.float32
    w_in = np.asarray(w_in, f)
    w_in_t = np.ascontiguousarray(w_in.reshape(DEPTH, 16, 128, 48, 128).transpose(0, 3, 2, 1, 4))

    def moving(w):
        return w.reshape(DEPTH, 4, 4, 128, 512).transpose(0, 1, 3, 2, 4)

    w_kv_t = np.ascontiguousarray(np.stack([moving(w_in[:, :, 3584:4096]), moving(w_in[:, :, 4096:4608])], axis=1))
    w_pw_t = np.ascontiguousarray(np.asarray(w_pw, f).reshape(DEPTH, 8, 128, 8, 128).transpose(0, 3, 2, 1, 4))
    w_out_t = np.ascontiguousarray(np.asarray(w_out, f).reshape(DEPTH, 16, 128, 16, 128).transpose(0, 3, 2, 1, 4))
    wm = np.asarray(w_mem_kv, f)
    w_mkv_t = np.ascontiguousarray(np.stack([moving(wm[:, :, 0:512]), moving(wm[:, :, 512:1024])], axis=1))
    convw_t = np.ascontiguousarray(np.asarray(conv_w, f).reshape(DEPTH, CW, 8, 128).transpose(3, 0, 2, 1))
    cw_t = np.asarray(conv_w, f).reshape(DEPTH, CW, 8, 128).transpose(0, 2, 3, 1)
    convd_t = np.zeros((DEPTH, 8, 128, CW, 128), f)
    ar = np.arange(128)
    convd_t[:, :, ar, :, ar] = cw_t.transpose(2, 0, 1, 3)
    cpar = np.stack([np.asarray(conv_b, f), np.asarray(conv_ln_g, f), np.asarray(conv_ln_b, f)], axis=1)
    cpar_t = np.ascontiguousarray(cpar.reshape(DEPTH, 3, 8, 128).transpose(3, 0, 1, 2))
    lpar = np.stack([np.asarray(ln_g, f), np.asarray(ln_b, f)], axis=1)
    lpar_t = np.ascontiguousarray(lpar.reshape(DEPTH, 2, 16, 128).transpose(3, 0, 1, 2))
    kl = np.arange(640)[:, None]
    q = np.arange(128)[None, :]
    idx = np.clip(q - (kl - 512), -128, 128) + 128
    valid = np.where(q < 64, kl < 576, kl >= 64)
    tab = np.asarray(rel_table, f)
    b = tab[:, :, idx]
    b = np.where(valid[None, None], b, f(-1e30))
    bias_t = np.ascontiguousarray(b.reshape(DEPTH, NH, 5, 128, 128).transpose(0, 3, 1, 2, 4))
    return dict(w_in_t=w_in_t, w_kv_t=w_kv_t, w_pw_t=w_pw_t, w_out_t=w_out_t, w_mkv_t=w_mkv_t, convw_t=convw_t,
                convd_t=convd_t, cpar_t=cpar_t, lpar_t=lpar_t, bias_t=bias_t, ident=np.eye(128, dtype=f))


_NC_CACHE = {}


def kernel(x_prompt, x_sample, mem_prompt, cache_conv, cache_att_k, cache_att_v, cache_mem_k, cache_mem_v,
           w_in, conv_w, conv_b, conv_ln_g, conv_ln_b, w_pw, rel_table, w_mem_kv, w_out, ln_g, ln_b):
    f = np.float32
    shared = _prep_weights(w_in, conv_w, conv_b, conv_ln_g, conv_ln_b, w_pw, rel_table, w_mem_kv, w_out, ln_g, ln_b)
    x_prompt = np.asarray(x_prompt, f)
    x_sample = np.asarray(x_sample, f)
    mem_prompt = np.asarray(mem_prompt, f)
    cache_conv = np.asarray(cache_conv, f)
    cak = np.asarray(cache_att_k, f).reshape(DEPTH, 16, 512, 512)
    cav = np.asarray(cache_att_v, f).reshape(DEPTH, 16, 512, 512)
    cmk = np.asarray(cache_mem_k, f).reshape(DEPTH, 16, NMEM, 512)
    cmv = np.asarray(cache_mem_v, f).reshape(DEPTH, 16, NMEM, 512)
    in_maps = []
    for i in range(NCORES):
        m = dict(shared)
        m["xp"] = np.ascontiguousarray(x_prompt[i])
        m["xs"] = np.ascontiguousarray(x_sample[2 * i:2 * i + 2].reshape(2 * DEC_SEQ, D))
        m["memp"] = np.ascontiguousarray(mem_prompt[i])
        m["cconv"] = np.ascontiguousarray(cache_conv[:, 2 * i:2 * i + 2])
        m["catk"] = np.ascontiguousarray(cak[:, 2 * i:2 * i + 2])
        m["catv"] = np.ascontiguousarray(cav[:, 2 * i:2 * i + 2])
        m["cmk"] = np.ascontiguousarray(cmk[:, 2 * i:2 * i + 2])
        m["cmv"] = np.ascontiguousarray(cmv[:, 2 * i:2 * i + 2])
        in_maps.append(m)
    if "nc" not in _NC_CACHE:
        _NC_CACHE["nc"] = build_program()
    nc = _NC_CACHE["nc"]
    res = run_bass_kernel_spmd(nc, in_maps, core_ids=list(range(NCORES)))
    R = res.results
    y_p = np.stack([R[i]["y_p"] for i in range(NCORES)], axis=0)
    y_s = np.stack([R[i]["y_s"] for i in range(NCORES)], axis=0).reshape(16, DEC_SEQ, D)
    ncp = np.stack([R[i]["ncp"] for i in range(NCORES)], axis=1)
    nkp = np.stack([R[i]["nkp"] for i in range(NCORES)], axis=1).reshape(DEPTH, 8, 512, NH, HD)
    nvp = np.stack([R[i]["nvp"] for i in range(NCORES)], axis=1).reshape(DEPTH, 8, 512, NH, HD)
    nmk = np.stack([R[i]["nmk"] for i in range(NCORES)], axis=1).reshape(DEPTH, 8, NMEM, NH, HD)
    nmv = np.stack([R[i]["nmv"] for i in range(NCORES)], axis=1).reshape(DEPTH, 8, NMEM, NH, HD)
    ncs = np.concatenate([R[i]["ncs"] for i in range(NCORES)], axis=1)
    nks = np.concatenate([R[i]["nks"] for i in range(NCORES)], axis=1).reshape(DEPTH, 16, DEC_SEQ, NH, HD)
    nvs = np.concatenate([R[i]["nvs"] for i in range(NCORES)], axis=1).reshape(DEPTH, 16, DEC_SEQ, NH, HD)
    return (y_p.astype(f), y_s.astype(f), ncp.astype(f), nkp.astype(f), nvp.astype(f), nmk.astype(f),
            nmv.astype(f), ncs.astype(f), nks.astype(f), nvs.astype(f))
```

```python
import numpy as np
import concourse.bass as bass
import concourse.mybir as mybir
from concourse.bass_utils import run_bass_kernel_spmd

F32 = mybir.dt.float32
F32R = mybir.dt.float32r
BF16 = mybir.dt.bfloat16
AF = mybir.ActivationFunctionType
ALU = mybir.AluOpType

D = 2048
SEQ = 2048
DEPTH = 4
NCORES = 8
DEC_SEQ = 64
CW = 31
DC = 1024
HD = 128
NH = 4
NMEM = 256
DIN = 6144
ALPHA = float((2 * DEPTH) ** 0.25)
EPS = 1e-5
SCALE = float(HD ** -0.5)
TBP = 512
NBLK = SEQ // TBP
EXP_CLAMP = 60.0

MM_DT = BF16
ATT_DT = BF16
N_LAYERS = DEPTH
STOP_AT = 99
ABL = set()
NW_SLOTS = 6
USE_WCACHE = True
RAW_ONLY_SAME_ENG = False
FUSE_WAIT = True

ENGS = ("pe", "act", "dve", "pool", "sp")


class Res:
    __slots__ = ("name", "lw", "rd", "excl")

    def __init__(self, name="", excl=False):
        self.name = name
        self.lw = None
        self.rd = []
        self.excl = excl


class Op:
    __slots__ = ("eng", "fn", "deps", "eidx", "need_inc", "count", "dma", "dsem", "dcount", "waits", "vc", "tag", "gidx")


class Prog:
    def __init__(self, nc, n_dma_sems=8, same_eng_dist=10 ** 9):
        self.nc = nc
        self.ops = []
        self.eng_ops = {e: [] for e in ENGS}
        self.n_dma_sems = n_dma_sems
        self.same_eng_dist = same_eng_dist
        self.tag = ""

    def add(self, eng, fn, reads=(), writes=(), dma=False):
        op = Op()
        op.eng = eng
        op.fn = fn
        op.dma = dma
        op.need_inc = False
        op.count = 0
        op.dsem = None
        op.dcount = 0
        op.waits = []
        op.vc = None
        op.tag = self.tag
        deps = []
        seen = set()
        for r in reads:
            a = r.lw
            if a is not None and id(a) not in seen:
                seen.add(id(a))
                deps.append((a, True))
            if r.excl:
                last = {}
                for a in r.rd:
                    if a.eng != eng:
                        last[a.eng] = a
                for a in last.values():
                    if id(a) not in seen:
                        seen.add(id(a))
                        deps.append((a, False))
        for w in writes:
            a = w.lw
            if a is not None and id(a) not in seen:
                seen.add(id(a))
                deps.append((a, False))
            last = {}
            for a in w.rd:
                if a.dma:
                    if id(a) not in seen:
                        seen.add(id(a))
                        deps.append((a, False))
                else:
                    last[a.eng] = a
            for a in last.values():
                if id(a) not in seen:
                    seen.add(id(a))
                    deps.append((a, False))
        for r in reads:
            r.rd.append(op)
        for w in writes:
            w.lw = op
            w.rd = []
        op.deps = deps
        op.eidx = len(self.eng_ops[eng])
        op.gidx = len(self.ops)
        self.eng_ops[eng].append(op)
        self.ops.append(op)
        return op

    def resolve(self):
        nc = self.nc
        for op in self.ops:
            kept = []
            for a, raw in op.deps:
                if a is op:
                    continue
                if a.dma:
                    kept.append(a)
                    continue
                if a.eng == op.eng:
                    if op.dma:
                        a.need_inc = True
                        kept.append(a)
                        continue
                    if op.eng == "pe":
                        continue
                    if op.eidx - a.eidx <= self.same_eng_dist and (raw or not RAW_ONLY_SAME_ENG):
                        a.need_inc = True
                        kept.append(a)
                    continue
                a.need_inc = True
                kept.append(a)
            op.deps = kept
        self.sems = {e: nc.alloc_semaphore("s_" + e) for e in ENGS}
        self.dma_sems = {}
        dma_state = {}
        for e in ENGS:
            c = 0
            nd = 0
            for op in self.eng_ops[e]:
                if op.dma:
                    k = nd % self.n_dma_sems
                    nd += 1
                    key = (e, k)
                    if key not in self.dma_sems:
                        self.dma_sems[key] = nc.alloc_semaphore("d_%s%d" % (e, k))
                        dma_state[key] = 0
                    dma_state[key] += 16
                    op.dsem = key
                    op.dcount = dma_state[key]
                elif op.need_inc:
                    c += 1
                    op.count = c
        self.final_dma = dict(dma_state)
        know = {e: {} for e in ENGS}
        for op in self.ops:
            kn = know[op.eng]
            waits = []
            if op.dma and op.dcount > 16:
                if kn.get(op.dsem, 0) < op.dcount - 16:
                    waits.append((op.dsem, op.dcount - 16))
                    kn[op.dsem] = op.dcount - 16
            for a in sorted(op.deps, key=lambda d: -d.gidx):
                if a.dma:
                    key, cnt = a.dsem, a.dcount
                else:
                    key, cnt = a.eng, a.count
                if kn.get(key, 0) >= cnt:
                    continue
                waits.append((key, cnt))
                kn[key] = cnt
                if a.vc is not None:
                    for k2, v2 in a.vc.items():
                        if kn.get(k2, 0) < v2:
                            kn[k2] = v2
            op.waits = waits
            if op.dma:
                vc = dict(kn)
                vc[op.dsem] = op.dcount
                op.vc = vc
            elif op.need_inc:
                vc = dict(kn)
                vc[op.eng] = op.count
                op.vc = vc

    def _sem(self, key):
        if isinstance(key, tuple):
            return self.dma_sems[key]
        return self.sems[key]

    def emit(self, block):
        self.resolve()

        def run(engname, eng):
            for op in self.eng_ops[engname]:
                ws = op.waits
                if FUSE_WAIT and ws:
                    for key, cnt in ws[:-1]:
                        eng.wait_ge(self._sem(key), cnt)
                    ins = op.fn(eng)
                    ins._wait_ge(self._sem(ws[-1][0]), ws[-1][1])
                else:
                    for key, cnt in ws:
                        eng.wait_ge(self._sem(key), cnt)
                    ins = op.fn(eng)
                if op.dma:
                    ins.then_inc(self.dma_sems[op.dsem], 16)
                elif op.need_inc:
                    ins.then_inc(self.sems[engname], 1)
            for key, cnt in self.final_dma.items():
                if key[0] == engname:
                    eng.wait_ge(self.dma_sems[key], cnt)

        if self.eng_ops["sp"]:
            block.sync(lambda e: run("sp", e))
        if self.eng_ops["pe"]:
            block.tensor(lambda e: run("pe", e))
        if self.eng_ops["act"]:
            block.scalar(lambda e: run("act", e))
        if self.eng_ops["dve"]:
            block.vector(lambda e: run("dve", e))
        if self.eng_ops["pool"]:
            block.gpsimd(lambda e: run("pool", e))


_DBG = {}


class Ring:
    def __init__(self, items):
        self.items = items
        self.i = 0

    def next(self):
        it = self.items[self.i % len(self.items)]
        self.i += 1
        return it


def build_program():
    nc = bass.Bass("TRN2", target_bir_lowering=False)
    L = N_LAYERS

    def din(name, shape, dt=F32):
        return nc.dram_tensor(name, list(shape), dt, kind="ExternalInput").ap()

    def dout(name, shape):
        return nc.dram_tensor(name, list(shape), F32, kind="ExternalOutput").ap()

    xp_d = din("xp", [SEQ, D])
    xs_d = din("xs", [2 * DEC_SEQ, D])
    memp_d = din("memp", [NMEM, D])
    cconv_d = din("cconv", [DEPTH, 2, CW - 1, DC])
    catk_d = din("catk", [DEPTH, 2, 512, 512])
    catv_d = din("catv", [DEPTH, 2, 512, 512])
    cmk_d = din("cmk", [DEPTH, 2, NMEM, 512])
    cmv_d = din("cmv", [DEPTH, 2, NMEM, 512])
    w_in_d = din("w_in_t", [DEPTH, 48, 128, 16, 128])
    w_kv_d = din("w_kv_t", [DEPTH, 2, 4, 128, 4, 512])
    w_pw_d = din("w_pw_t", [DEPTH, 8, 128, 8, 128])
    w_out_d = din("w_out_t", [DEPTH, 16, 128, 16, 128])
    w_mkv_d = din("w_mkv_t", [DEPTH, 2, 4, 128, 4, 512])
    convd_d = din("convd_t", [DEPTH, 8, 128, CW, 128])
    convw_d = din("convw_t", [128, DEPTH, 8, CW])
    cpar_d = din("cpar_t", [128, DEPTH, 3, 8])
    lpar_d = din("lpar_t", [128, DEPTH, 2, 16])
    bias_d = din("bias_t", [DEPTH, 128, NH, 5, 128])
    ident_d = din("ident", [128, 128])
    yp_d = dout("y_p", [SEQ, D])
    ys_d = dout("y_s", [2 * DEC_SEQ, D])
    ncp_d = dout("ncp", [DEPTH, CW - 1, DC])
    nkp_d = dout("nkp", [DEPTH, 512, 512])
    nvp_d = dout("nvp", [DEPTH, 512, 512])
    nmk_d = dout("nmk", [DEPTH, NMEM, 512])
    nmv_d = dout("nmv", [DEPTH, NMEM, 512])
    ncs_d = dout("ncs", [DEPTH, 2, CW - 1, DC])
    nks_d = dout("nks", [DEPTH, 2, DEC_SEQ, 512])
    nvs_d = dout("nvs", [DEPTH, 2, DEC_SEQ, 512])

    def dscratch(name, shape):
        return nc.dram_tensor(name, list(shape), MM_DT, kind="Internal").ap()

    wc_in = dscratch("wc_in", [DEPTH, 48, 128, 2048])
    wc_kv = dscratch("wc_kv", [DEPTH, 2, 4, 128, 2048])
    wc_pw = dscratch("wc_pw", [DEPTH, 8, 128, 1024])
    wc_out = dscratch("wc_out", [DEPTH, 16, 128, 2048])
    wc_cv = dscratch("wc_cv", [DEPTH, 8, 2, 128, 2048])
    wcache = {}

    P = Prog(nc)
    _DBG["P"] = P
    sb = nc.alloc_sbuf_tensor

    x32 = sb("x32", [128, 16, TBP], F32)
    r_x32 = [Res("x32_%d" % c) for c in range(16)]
    xb = sb("xb", [128, 16, TBP], MM_DT)
    r_xb = [Res("xb%d" % c) for c in range(16)]
    ybuf = sb("ybuf", [128, 8, TBP], F32)
    mix = ybuf[:, :, :].bitcast(MM_DT).rearrange("p a (b c) -> p (a b) c", b=2)
    r_mix = [Res("mix%d" % c) for c in range(16)]

    def r_y(j):
        return [r_mix[2 * j], r_mix[2 * j + 1]]

    sact = sb("s_act", [128, 8, TBP], MM_DT)
    r_s = [Res("s%d" % c) for c in range(8)]
    UW = CW - 1 + TBP
    ub = sb("ub", [128, 8, UW], MM_DT)
    r_u = [Res("u%d" % c) for c in range(8)]
    ctail = [sb("ctail%d" % l, [128, 8, CW - 1], F32) for l in range(L)]
    r_ctail = [Res("ctail%d" % l) for l in range(L)]
    tail_s = sb("tail_s", [128, 8, 2 * (CW - 1)], F32)
    r_tail_s = Res("tail_s")
    NKB = max(L + 1, 3)
    kbuf = [sb("kbuf%d" % i, [128, NH, 512], ATT_DT) for i in range(NKB)]
    r_kbuf = [[Res("k%d_%d" % (i, h)) for h in range(NH)] for i in range(NKB)]
    vbuf = [sb("vbuf%d" % i, [128, 4, 512], ATT_DT) for i in range(NKB)]
    r_vbuf = [[Res("v%d_%d" % (i, t)) for t in range(4)] for i in range(NKB)]
    qt = sb("qt", [128, NH, TBP], ATT_DT)
    r_q = [Res("q%d" % h) for h in range(NH)]
    mkt = [sb("mkt%d" % l, [128, NH, NMEM], ATT_DT) for l in range(L)]
    r_mkt = [Res("mkt%d" % l) for l in range(L)]
    mv = [sb("mv%d" % l, [128, 2, 512], ATT_DT) for l in range(L)]
    r_mv = [Res("mv%d" % l) for l in range(L)]
    NW = NW_SLOTS
    wring = Ring([(sb("wr%d" % i, [128, 2048], MM_DT), Res("wr%d" % i)) for i in range(NW)])
    tmpA = Ring([(sb("tA%d" % i, [128, TBP], F32), Res("tA%d" % i)) for i in range(3)])
    ptr = Ring([(sb("pt%d" % i, [128, 640], ATT_DT), Res("pt%d" % i)) for i in range(4)])
    stat = [sb("st%d" % i, [128, TBP], F32) for i in range(3)]
    r_stat = [Res("st%d" % i) for i in range(3)]
    stg = Ring([(sb("stg%d" % i, [128, 1024], F32), Res("stg%d" % i)) for i in range(2)])
    bias_sb = sb("bias_sb", [128, NH, 5, 128], ATT_DT)
    r_bias = Res("bias")
    convw = sb("convw", [128, DEPTH, 8, CW], F32)
    cpar = sb("cpar", [128, DEPTH, 3, 8], F32)
    lpar = sb("lpar", [128, DEPTH, 2, 16], F32)
    ident = sb("ident_sb", [128, 128], F32)
    ones = sb("ones_sb", [128, 128], F32)
    ones_a = sb("ones_a", [128, 128], ATT_DT)
    ident_a = sb("ident_a", [128, 128], ATT_DT)
    epsc = sb("epsc", [128, 1], F32)
    r_const = Res("const")
    ps = [nc.alloc_psum_tensor("ps%d" % i, [128, 512], F32) for i in range(8)]
    r_ps = [Res("ps%d" % i, excl=True) for i in range(8)]

    def dma(q, out, in_, reads=(), writes=()):
        P.add(q, lambda e: e.dma_start(out=out, in_=in_), reads=reads, writes=writes, dma=True)

    def mm(out, lhsT, rhs, start, stop, reads, writes):
        P.add("pe", lambda e: e.matmul(out, lhsT=lhsT, rhs=rhs, start=start, stop=stop), reads=reads, writes=writes)

    def tr(out, in_, k, reads, writes):
        P.add("pe", lambda e: e.transpose(out=out, in_=in_, identity=ident[0:k, 0:k]), reads=list(reads) + [r_const],
              writes=writes)

    def act(out, in_, func, reads, writes, scale=1.0, bias=0.0):
        P.add("act", lambda e: e.activation(out=out, in_=in_, func=func, bias=bias, scale=scale), reads=reads,
              writes=writes)

    def tt(out, in0, in1, op, reads, writes, eng="dve"):
        P.add(eng, lambda e: e.tensor_tensor(out=out, in0=in0, in1=in1, op=op), reads=reads, writes=writes)

    def stt(out, in0, scalar, in1, op0, op1, reads, writes, eng="dve"):
        P.add(eng, lambda e: e.scalar_tensor_tensor(out=out, in0=in0, scalar=scalar, in1=in1, op0=op0, op1=op1),
              reads=reads, writes=writes)

    def ts(out, in0, s1, s2, op0, op1, reads, writes, eng="dve"):
        if s2 is None:
            P.add(eng, lambda e: e.tensor_scalar(out=out, in0=in0, scalar1=s1, scalar2=None, op0=op0), reads=reads,
                  writes=writes)
        else:
            P.add(eng, lambda e: e.tensor_scalar(out=out, in0=in0, scalar1=s1, scalar2=s2, op0=op0, op1=op1),
                  reads=reads, writes=writes)

    def cp(out, in_, reads, writes, eng="dve"):
        P.add(eng, lambda e: e.tensor_copy(out=out, in_=in_), reads=reads, writes=writes)

    def recip(out, in_, reads, writes):
        P.add("dve", lambda e: e.reciprocal(out=out, in_=in_), reads=reads, writes=writes)

    def load_w(src_ap, shape3, cache=None):
        t, r = wring.next()
        a, b_ = shape3
        n = a * b_
        view = t[:, 0:n].rearrange("p (a b) -> p a b", a=a)
        if cache is None or not USE_WCACHE:
            dma("pool", view, src_ap, writes=[r])
            return view, r
        key, cap = cache
        if key not in wcache:
            rc = Res("wc")
            wcache[key] = rc
            dma("pool", view, src_ap, writes=[r])
            dma("sp", cap, t[:, 0:n], reads=[r], writes=[rc])
        else:
            dma("pool", t[:, 0:n], cap, reads=[wcache[key]], writes=[r])
        return view, r

    def bank4(bk, n):
        return ps[bk][:, :].rearrange("p (a b) -> p a b", a=4)[:, :, 0:n]

    dma("sp", ident[:], ident_d, writes=[r_const])
    dma("sp", convw[:], convw_d, writes=[r_const])
    dma("sp", cpar[:], cpar_d, writes=[r_const])
    dma("sp", lpar[:], lpar_d, writes=[r_const])
    P.add("dve", lambda e: e.memset(ones[:], 1.0), writes=[r_const])
    P.add("dve", lambda e: e.memset(ones_a[:], 1.0), writes=[r_const])
    cp(ident_a[:], ident[:], [r_const], [r_const])
    P.add("dve", lambda e: e.memset(epsc[:], EPS), writes=[r_const])

    def load_tokens(src_rows_ap, ntok, want32):
        for t0 in range(0, ntok, 128):
            n = min(128, ntok - t0)
            for hf in range(2):
                st, r_st = stg.next()
                dma("sp", st[0:n, :], src_rows_ap[t0:t0 + n, hf * 1024:(hf + 1) * 1024], writes=[r_st])
                for g in range(2):
                    bk = 6 + g
                    c0 = hf * 8 + g * 4
                    for cc in range(4):
                        tr(ps[bk][:, cc * 128:cc * 128 + n], st[0:n, (g * 4 + cc) * 128:(g * 4 + cc + 1) * 128], n,
                           [r_st], [r_ps[bk]])
                    src = bank4(bk, n)
                    act(xb[:, c0:c0 + 4, t0:t0 + n], src, AF.Identity, [r_ps[bk]], r_xb[c0:c0 + 4])
                    if want32:
                        cp(x32[:, c0:c0 + 4, t0:t0 + n], src, [r_ps[bk]], r_x32[c0:c0 + 4])

    def tokmajor_proj(groups, w_src, evac, ckey=None, csrc=None):
        assert len(groups) <= 4
        for pc in range(4):
            wv, r_w = load_w(w_src[pc], (4, 512), None if ckey is None else (ckey + (pc,), csrc[pc]))
            for i, (c0, ncol) in enumerate(groups):
                bk = 4 + i
                for cc in range(4):
                    c = pc * 4 + cc
                    mm(ps[bk][0:ncol, :], xb[:, c, c0:c0 + ncol], wv[:, cc, :], c == 0, c == 15,
                       [r_w, r_xb[c]], [r_ps[bk]])
        for i, (c0, ncol) in enumerate(groups):
            evac(i, ps[4 + i][0:ncol, :], r_ps[4 + i])

    def mem_kv_from_staging(l, m, st, r_st):
        cp(mv[l][:, m, :], st[:, 512:1024], [r_st], [r_mv[l]])
        bk = 6 + (m % 2)
        for h in range(NH):
            tr(ps[bk][:, h * 128:(h + 1) * 128], st[:, h * 128:(h + 1) * 128], 128, [r_st], [r_ps[bk]])
        act(mkt[l][:, :, m * 128:(m + 1) * 128], bank4(bk, 128), AF.Identity, [r_ps[bk]], [r_mkt[l]])

    load_tokens(memp_d, NMEM, False)
    for l in range(L):
        stk = [None, None]

        def ev_k(i, pap, rp, l=l, stk=stk):
            st, r_st = stg.next()
            cp(st[:, 0:512], pap, [rp], [r_st])
            dma("sp", nmk_d[l, i * 128:(i + 1) * 128, :], st[:, 0:512], reads=[r_st])
            stk[i] = (st, r_st)

        tokmajor_proj([(0, 128), (128, 128)], w_mkv_d[l, 0], ev_k)

        def ev_v(i, pap, rp, l=l, stk=stk):
            st, r_st = stk[i]
            act(st[:, 512:1024], pap, AF.Identity, [rp], [r_st])
            dma("sp", nmv_d[l, i * 128:(i + 1) * 128, :], st[:, 512:1024], reads=[r_st])

        tokmajor_proj([(0, 128), (128, 128)], w_mkv_d[l, 1], ev_v)
        for m in range(2):
            st, r_st = stk[m]
            mem_kv_from_staging(l, m, st, r_st)

    kfree = list(range(NKB))
    kstate = [None] * L

    def layer_block(l, TB, segs, blk, is_sample):
        NSEG = len(segs)
        SL = segs[0][1]
        SEGW = CW - 1 + SL
        NT = (TB + 127) // 128
        last_prompt = (not is_sample) and blk == NBLK - 1
        want_kv_out = is_sample or last_prompt
        uview = ub[:, :, 0:NSEG * SEGW].rearrange("p c (s w) -> p c s w", s=NSEG)

        def inproj(j, bk):
            wv, r_w = load_w(w_in_d[l, j], (16, 128), (("in", l, j), wc_in[l, j]))
            for c in range(16):
                mm(ps[bk][:, 0:TB], wv[:, c, :], xb[:, c, 0:TB], c == 0, c == 15, [r_w, r_xb[c]], [r_ps[bk]])

        def stat_tile(ap_, rr, first, last, b1, b2):
            tA, r_tA = tmpA.next()
            sq = tA[:, :].bitcast(ATT_DT)[:, 0:TB]
            act(sq, ap_, AF.Square, rr, [r_tA])
            mm(ps[b1][:, 0:TB], ones[:, :], ap_, first, last, [r_const] + rr, [r_ps[b1]])
            mm(ps[b2][:, 0:TB], ones_a[:, :], sq, first, last, [r_const, r_tA], [r_ps[b2]])

        def stat_finish(nfeat, b1, b2):
            inv = 1.0 / nfeat
            act(stat[1][:, 0:TB], ps[b1][:, 0:TB], AF.Square, [r_ps[b1]], [r_stat[1]], scale=inv)
            ts(stat[0][:, 0:TB], ps[b1][:, 0:TB], inv, None, ALU.mult, None, [r_ps[b1]], [r_stat[0]])
            stt(stat[1][:, 0:TB], ps[b2][:, 0:TB], inv, stat[1][:, 0:TB], ALU.mult, ALU.subtract,
                [r_ps[b2], r_stat[1]], [r_stat[1]])
            act(stat[2][:, 0:TB], stat[1][:, 0:TB], AF.Sqrt, [r_stat[1], r_const], [r_stat[2]],
                bias=epsc[:, 0:1])
            recip(stat[2][:, 0:TB], stat[2][:, 0:TB], [r_stat[2]], [r_stat[2]])
            stt(stat[0][:, 0:TB], stat[0][:, 0:TB], -1.0, stat[2][:, 0:TB], ALU.mult, ALU.mult,
                [r_stat[0], r_stat[2]], [r_stat[0]])

        dma("pool", bias_sb[:], bias_d[l], writes=[r_bias])

        P.tag = 'S1_glu_conv'
        for s in range(NSEG):
            if is_sample:
                st, r_st = stg.next()
                dma("sp", st[0:CW - 1, :], cconv_d[l, s], writes=[r_st])
                for g in range(2):
                    bk = 6 + g
                    for jj in range(4):
                        j = g * 4 + jj
                        tr(ps[bk][:, jj * (CW - 1):(jj + 1) * (CW - 1)], st[0:CW - 1, j * 128:(j + 1) * 128],
                           CW - 1, [r_st], [r_ps[bk]])
                    cp(uview[:, g * 4:(g + 1) * 4, s, 0:CW - 1],
                       ps[bk][:, 0:4 * (CW - 1)].rearrange("p (a b) -> p a b", a=4), [r_ps[bk]],
                       r_u[g * 4:(g + 1) * 4])
            elif blk == 0:
                P.add("dve", lambda e: e.memset(uview[:, :, 0, 0:CW - 1], 0.0), writes=r_u)
            else:
                cp(uview[:, :, 0, 0:CW - 1], ctail[l][:, :, :], [r_ctail[l]], r_u)
        tailbuf, r_tail = (tail_s, r_tail_s) if is_sample else (ctail[l], r_ctail[l])

        def glu(j):
            ba, bb = 2 * (j % 2), 2 * (j % 2) + 1
            inproj(j, ba)
            inproj(8 + j, bb)
            tA, r_tA = tmpA.next()
            act(tA[:, 0:TB], ps[bb][:, 0:TB], AF.Sigmoid, [r_ps[bb]], [r_tA])
            for s in range(NSEG):
                c0 = segs[s][0]
                tt(uview[:, j, s, CW - 1:SEGW], ps[ba][:, c0:c0 + SL], tA[:, c0:c0 + SL], ALU.mult,
                   [r_ps[ba], r_tA], [r_u[j]])
                tt(tailbuf[:, j, s * (CW - 1):(s + 1) * (CW - 1)], ps[ba][:, c0 + SL - (CW - 1):c0 + SL],
                   tA[:, c0 + SL - (CW - 1):c0 + SL], ALU.mult, [r_ps[ba], r_tA], [r_tail])

        def conv(j):
            bk = 4 + (j % 2)
            wa, r_wa = load_w(convd_d[l, j, :, 0:16, :], (16, 128), (("cv", l, j, 0), wc_cv[l, j, 0]))
            wb, r_wb = load_w(convd_d[l, j, :, 16:CW, :], (CW - 16, 128),
                              (("cv", l, j, 1), wc_cv[l, j, 1][:, 0:(CW - 16) * 128]))
            if NSEG == 1:
                for k in range(CW):
                    wv, r_w = (wa, r_wa) if k < 16 else (wb, r_wb)
                    mm(ps[bk][:, 0:SL], wv[:, k % 16, :], uview[:, j, 0, k:k + SL], k == 0, k == CW - 1,
                       [r_w, r_u[j]], [r_ps[bk]])
            else:
                o3 = ps[bk][:, 0:TB].rearrange("p (s q) -> p s q", s=NSEG)
                for k in range(CW):
                    wv, r_w = (wa, r_wa) if k < 16 else (wb, r_wb)
                    mm(o3, wv[:, k % 16, :], uview[:, j, :, k:k + SL], k == 0, k == CW - 1,
                       [r_w, r_u[j]], [r_ps[bk]])
            act(ybuf[:, j, 0:TB], ps[bk][:, 0:TB], AF.Identity, [r_ps[bk], r_const], r_y(j),
                bias=cpar[:, l, 0, j:j + 1])

        def glu_evac(j):
            ba, bb = 2 * (j % 2), 2 * (j % 2) + 1
            tA, r_tA = tmpA.next()
            act(tA[:, 0:TB], ps[bb][:, 0:TB], AF.Sigmoid, [r_ps[bb]], [r_tA])
            for s in range(NSEG):
                c0 = segs[s][0]
                tt(uview[:, j, s, CW - 1:SEGW], ps[ba][:, c0:c0 + SL], tA[:, c0:c0 + SL], ALU.mult,
                   [r_ps[ba], r_tA], [r_u[j]])
                tt(tailbuf[:, j, s * (CW - 1):(s + 1) * (CW - 1)], ps[ba][:, c0 + SL - (CW - 1):c0 + SL],
                   tA[:, c0 + SL - (CW - 1):c0 + SL], ALU.mult, [r_ps[ba], r_tA], [r_tail])

        wl = [load_w(w_in_d[l, jj], (16, 128), (("in", l, jj), wc_in[l, jj])) for jj in (0, 8, 1, 9, 2, 10)]
        for c in range(16):
            for t_, (wv, r_w) in enumerate(wl):
                mm(ps[t_][:, 0:TB], wv[:, c, :], xb[:, c, 0:TB], c == 0, c == 15, [r_w, r_xb[c]], [r_ps[t_]])
        glu_evac(0)
        glu_evac(1)
        tA, r_tA = tmpA.next()
        act(tA[:, 0:TB], ps[5][:, 0:TB], AF.Sigmoid, [r_ps[5]], [r_tA])
        for s in range(NSEG):
            c0 = segs[s][0]
            tt(uview[:, 2, s, CW - 1:SEGW], ps[4][:, c0:c0 + SL], tA[:, c0:c0 + SL], ALU.mult,
               [r_ps[4], r_tA], [r_u[2]])
            tt(tailbuf[:, 2, s * (CW - 1):(s + 1) * (CW - 1)], ps[4][:, c0 + SL - (CW - 1):c0 + SL],
               tA[:, c0 + SL - (CW - 1):c0 + SL], ALU.mult, [r_ps[4], r_tA], [r_tail])
        for j in range(8):
            if 2 <= j and j + 1 < 8:
                glu(j + 1)
            conv(j)
            if j >= 1:
                stat_tile(ybuf[:, j - 1, 0:TB], r_y(j - 1), j == 1, False, 6, 7)
        stat_tile(ybuf[:, 7, 0:TB], r_y(7), False, True, 6, 7)
        if want_kv_out:
            for s in range(NSEG):
                st, r_st = stg.next()
                for g in range(2):
                    bk = 0 + g
                    for jj in range(4):
                        j = g * 4 + jj
                        tr(ps[bk][0:CW - 1, jj * 128:(jj + 1) * 128],
                           tailbuf[:, j, s * (CW - 1):(s + 1) * (CW - 1)], 128, [r_tail], [r_ps[bk]])
                    cp(st[0:CW - 1, g * 512:(g + 1) * 512], ps[bk][0:CW - 1, :], [r_ps[bk]], [r_st])
                dst = ncs_d[l, s] if is_sample else ncp_d[l]
                dma("sp", dst, st[0:CW - 1, :], reads=[r_st])

        if STOP_AT <= 1:
            return
        P.tag = 'S3_convln'
        stat_finish(DC, 6, 7)
        LAG = 2
        for jj in range(8 + 2 * LAG):
            j = jj
            if j < 8:
                tt(ybuf[:, j, 0:TB], ybuf[:, j, 0:TB], stat[2][:, 0:TB], ALU.mult, r_y(j) + [r_stat[2]], r_y(j))
            j = jj - LAG
            if 0 <= j < 8:
                tt(ybuf[:, j, 0:TB], ybuf[:, j, 0:TB], stat[0][:, 0:TB], ALU.add, r_y(j) + [r_stat[0]], r_y(j))
            j = jj - 2 * LAG
            if 0 <= j < 8:
                act(sact[:, j, 0:TB], ybuf[:, j, 0:TB], AF.Silu, r_y(j) + [r_const], [r_s[j]],
                    scale=cpar[:, l, 1, j:j + 1], bias=cpar[:, l, 2, j:j + 1])

        if STOP_AT <= 3:
            return
        P.tag = 'S5_qkv'
        kc = kfree.pop(0)
        for h in range(NH):
            bk = h % 4
            inproj(24 + h, bk)
            act(qt[:, h, 0:TB], ps[bk][:, 0:TB], AF.Identity, [r_ps[bk]], [r_q[h]], scale=SCALE)
        for h in range(NH):
            bk = h % 4
            inproj(28 + h, bk)
            cp(kbuf[kc][:, h, 0:TB], ps[bk][:, 0:TB], [r_ps[bk]], [r_kbuf[kc][h]])

        if is_sample:
            groups = [(segs[s][0], SL) for s in range(NSEG)]
        else:
            groups = [(t * 128, 128) for t in range(NT)]

        def ev_v(i, pap, rp):
            n = groups[i][1]
            cp(vbuf[kc][0:n, i, :], pap, [rp], [r_vbuf[kc][i]])
            if want_kv_out:
                st, r_st = stg.next()
                act(st[0:n, 0:512], pap, AF.Identity, [rp], [r_st])
                dst = nvs_d[l, i] if is_sample else nvp_d[l, i * 128:(i + 1) * 128, :]
                dma("sp", dst, st[0:n, 0:512], reads=[r_st])

        tokmajor_proj(groups, w_kv_d[l, 1], ev_v, ("kv", l, 1), wc_kv[l, 1])
        if want_kv_out:
            def ev_k(i, pap, rp):
                n = groups[i][1]
                st, r_st = stg.next()
                act(st[0:n, 0:512], pap, AF.Identity, [rp], [r_st])
                dst = nks_d[l, i] if is_sample else nkp_d[l, i * 128:(i + 1) * 128, :]
                dma("sp", dst, st[0:n, 0:512], reads=[r_st])

            tokmajor_proj(groups, w_kv_d[l, 0], ev_k, ("kv", l, 0), wc_kv[l, 0])

        if STOP_AT <= 2:
            return
        P.tag = 'S4_pw'
        for j in range(8):
            bp, bg = 2 * (j % 2), 2 * (j % 2) + 1
            wv, r_w = load_w(w_pw_d[l, j], (8, 128), (("pw", l, j), wc_pw[l, j]))
            for c in range(8):
                mm(ps[bp][:, 0:TB], wv[:, c, :], sact[:, c, 0:TB], c == 0, c == 7, [r_w, r_s[c]], [r_ps[bp]])
            inproj(16 + j, bg)
            tA, r_tA = tmpA.next()
            act(tA[:, 0:TB], ps[bg][:, 0:TB], AF.Silu, [r_ps[bg]], [r_tA])
            tt(mix[:, j, 0:TB], ps[bp][:, 0:TB], tA[:, 0:TB], ALU.mult, [r_ps[bp], r_tA], [r_mix[j]])

        if STOP_AT <= 4:
            return
        P.tag = 'S6_att'
        qgroups = []
        kcache = []
        if is_sample:
            for s in range(NSEG):
                kb = kfree.pop(0)
                kcache.append(kb)
                for t in range(4):
                    st, r_st = stg.next()
                    dma("sp", st[:, 0:512], catk_d[l, s, t * 128:(t + 1) * 128, :], writes=[r_st])
                    dma("sp", st[:, 512:1024], catv_d[l, s, t * 128:(t + 1) * 128, :], writes=[r_st])
                    bk = 6 + (t % 2)
                    for h in range(NH):
                        tr(ps[bk][:, h * 128:(h + 1) * 128], st[:, h * 128:(h + 1) * 128], 128, [r_st], [r_ps[bk]])
                    act(kbuf[kb][:, :, t * 128:(t + 1) * 128], bank4(bk, 128), AF.Identity, [r_ps[bk]], r_kbuf[kb])
                    cp(vbuf[kb][:, t, :], st[:, 512:1024], [r_st], [r_vbuf[kb][t]])
                tiles = [(kb, r * 128, 128, r, kb, r) for r in range(4)]
                tiles.append((kc, segs[s][0], SL, 4, kc, s))
                qgroups.append((segs[s][0], SL, tiles))
        else:
            kp = kstate[l]
            for j in range(NT):
                tiles = []
                for r in range(5):
                    t = j + r - 4
                    if blk * 4 + t < 0:
                        continue
                    if t < 0:
                        tiles.append((kp, (t + 4) * 128, 128, r, kp, t + 4))
                    else:
                        tiles.append((kc, t * 128, 128, r, kc, t))
                qgroups.append((j * 128, 128, tiles))

        units = [(gi, q0, nq, tiles, h) for gi, (q0, nq, tiles) in enumerate(qgroups) for h in range(NH)]
        pts = {}

        def banks_sc(u):
            return ((0, 1), (2, 3), (6, 7))[u % 3]

        def scores(u):
            gi, q0, nq, tiles, h = units[u]
            b0, b1 = banks_sc(u)
            for (kb, kcol, nk, r, vb, vt) in tiles:
                if r < 4:
                    o_ap, r_o = ps[b0][0:nk, r * 128:r * 128 + nq], r_ps[b0]
                else:
                    o_ap, r_o = ps[b1][0:nk, 0:nq], r_ps[b1]
                mm(o_ap, kbuf[kb][:, h, kcol:kcol + nk], qt[:, h, q0:q0 + nq], True, False,
                   [r_kbuf[kb][h], r_q[h]], [r_o])
                mm(o_ap, ident_a[0:nk, 0:nk], bias_sb[0:nk, h, r, 0:nq], False, True, [r_const, r_bias], [r_o])

        def softmax(u):
            gi, q0, nq, tiles, h = units[u]
            b0, b1 = banks_sc(u)
            pt, r_pt = ptr.next()
            pts[u] = (pt, r_pt)
            rs = [t[3] for t in tiles if t[3] < 4]
            if rs:
                rmin = min(rs)
                pv_ = ps[b0][:, :].rearrange("p (r q) -> p r q", q=128)[:, rmin:4, 0:nq]
                act(pt[:, 0:512].rearrange("p (r q) -> p r q", q=128)[:, rmin:4, 0:nq], pv_, AF.Exp, [r_ps[b0]], [r_pt])
            last = [t for t in tiles if t[3] == 4]
            if last:
                nk = last[0][2]
                act(pt[0:nk, 512:512 + nq], ps[b1][0:nk, 0:nq], AF.Exp, [r_ps[b1]], [r_pt])

        def pv(u):
            gi, q0, nq, tiles, h = units[u]
            bo, bl = 4, 5
            pt, r_pt = pts.pop(u)
            nt_ = len(tiles)
            for ti, (kb, kcol, nk, r, vb, vt) in enumerate(tiles):
                mm(ps[bo][:, h * 128:h * 128 + nq], vbuf[vb][0:nk, vt, h * 128:(h + 1) * 128],
                   pt[0:nk, r * 128:r * 128 + nq], ti == 0, ti == nt_ - 1, [r_vbuf[vb][vt], r_pt], [r_ps[bo]])
            for ti, (kb, kcol, nk, r, vb, vt) in enumerate(tiles):
                mm(ps[bl][:, h * 128:h * 128 + nq], ones_a[0:nk, :], pt[0:nk, r * 128:r * 128 + nq], ti == 0,
                   ti == nt_ - 1, [r_const, r_pt], [r_ps[bl]])
            if h == NH - 1:
                tA, r_tA = tmpA.next()
                t4 = tA[:, :].rearrange("p (a b) -> p a b", a=4)[:, :, 0:nq]
                recip(t4, bank4(bl, nq), [r_ps[bl]], [r_tA])
                tt(mix[:, 8:12, q0:q0 + nq], bank4(bo, nq), t4, ALU.mult, [r_ps[bo], r_tA], r_mix[8:12])

        nu = len(units)
        for u in range(min(2, nu)):
            scores(u)
        for u in range(nu):
            if u + 2 < nu:
                scores(u + 2)
            softmax(u)
            pv(u)
        P.tag = 'S6b_attgate'
        for h in range(NH):
            bk = h % 4
            inproj(36 + h, bk)
            tA, r_tA = tmpA.next()
            act(tA[:, 0:TB], ps[bk][:, 0:TB], AF.Silu, [r_ps[bk]], [r_tA])
            tt(mix[:, 8 + h, 0:TB], mix[:, 8 + h, 0:TB], tA[:, 0:TB], ALU.mult, [r_mix[8 + h], r_tA], [r_mix[8 + h]])

        if STOP_AT <= 5:
            return
        P.tag = 'S7_mem'
        for h in range(NH):
            bk = h % 4
            inproj(40 + h, bk)
            act(qt[:, h, 0:TB], ps[bk][:, 0:TB], AF.Identity, [r_ps[bk]], [r_q[h]], scale=SCALE)
        if is_sample:
            for s in range(NSEG):
                for m in range(2):
                    st, r_st = stg.next()
                    dma("sp", st[:, 0:512], cmk_d[l, s, m * 128:(m + 1) * 128, :], writes=[r_st])
                    dma("sp", st[:, 512:1024], cmv_d[l, s, m * 128:(m + 1) * 128, :], writes=[r_st])
                    mem_kv_from_staging(l, m, st, r_st)
                mem_attention(l, segs[s][0], SL)
        else:
            mem_attention(l, 0, TB)
        for h in range(NH):
            bk = h % 4
            inproj(44 + h, bk)
            tA, r_tA = tmpA.next()
            act(tA[:, 0:TB], ps[bk][:, 0:TB], AF.Silu, [r_ps[bk]], [r_tA])
            tt(mix[:, 12 + h, 0:TB], mix[:, 12 + h, 0:TB], tA[:, 0:TB], ALU.mult, [r_mix[12 + h], r_tA],
               [r_mix[12 + h]])

        if STOP_AT <= 6:
            return
        P.tag = 'S8_out'
        for j in range(16):
            bk = j % 4
            wv, r_w = load_w(w_out_d[l, j], (16, 128), (("out", l, j), wc_out[l, j]))
            for c in range(16):
                mm(ps[bk][:, 0:TB], wv[:, c, :], mix[:, c, 0:TB], c == 0, c == 15, [r_w, r_mix[c]], [r_ps[bk]])
            stt(x32[:, j, 0:TB], x32[:, j, 0:TB], ALPHA, ps[bk][:, 0:TB], ALU.mult, ALU.add, [r_x32[j], r_ps[bk]],
                [r_x32[j]])
            if j >= 2:
                stat_tile(x32[:, j - 2, 0:TB], [r_x32[j - 2]], j == 2, False, 4, 5)
        P.tag = 'S8_ln'
        stat_tile(x32[:, 14, 0:TB], [r_x32[14]], False, False, 4, 5)
        stat_tile(x32[:, 15, 0:TB], [r_x32[15]], False, True, 4, 5)
        stat_finish(D, 4, 5)
        final = (l == L - 1)
        LAG = 2
        for jj in range(16 + 2 * LAG):
            j = jj
            if j < 16:
                tt(x32[:, j, 0:TB], x32[:, j, 0:TB], stat[2][:, 0:TB], ALU.mult, [r_x32[j], r_stat[2]], [r_x32[j]])
            j = jj - LAG
            if 0 <= j < 16:
                tt(x32[:, j, 0:TB], x32[:, j, 0:TB], stat[0][:, 0:TB], ALU.add, [r_x32[j], r_stat[0]], [r_x32[j]])
            j = jj - 2 * LAG
            if 0 <= j < 16:
                if not final:
                    act(xb[:, j, 0:TB], x32[:, j, 0:TB], AF.Identity, [r_x32[j], r_const], [r_xb[j]],
                        scale=lpar[:, l, 0, j:j + 1], bias=lpar[:, l, 1, j:j + 1])
                act(x32[:, j, 0:TB], x32[:, j, 0:TB], AF.Identity, [r_x32[j], r_const], [r_x32[j]],
                    scale=lpar[:, l, 0, j:j + 1], bias=lpar[:, l, 1, j:j + 1])
        if final:
            for t0 in range(0, TB, 128):
                for hf in range(2):
                    st, r_st = stg.next()
                    for g in range(2):
                        bk = 6 + g
                        for cc in range(4):
                            c = hf * 8 + g * 4 + cc
                            tr(ps[bk][:, cc * 128:(cc + 1) * 128], x32[:, c, t0:t0 + 128], 128, [r_x32[c]],
                               [r_ps[bk]])
                        if g == 0:
                            cp(st[:, 0:512], ps[bk][:, :], [r_ps[bk]], [r_st])
                        else:
                            act(st[:, 512:1024], ps[bk][:, :], AF.Identity, [r_ps[bk]], [r_st])
                    if is_sample:
                        dma("sp", ys_d[t0:t0 + 128, hf * 1024:(hf + 1) * 1024], st[:, :], reads=[r_st])
                    else:
                        r0 = blk * TBP + t0
                        dma("sp", yp_d[r0:r0 + 128, hf * 1024:(hf + 1) * 1024], st[:, :], reads=[r_st])

        if is_sample:
            kfree.extend(kcache)
            kfree.append(kc)
        else:
            if kstate[l] is not None:
                kfree.append(kstate[l])
            kstate[l] = kc

    def mem_attention(l, q0, nq):
        pts = {}

        def sc(h):
            for m in range(2):
                bk = 2 * (h % 2) + m
                mm(ps[bk][:, 0:nq], mkt[l][:, h, m * 128:(m + 1) * 128], qt[:, h, q0:q0 + nq], True, True,
                   [r_mkt[l], r_q[h]], [r_ps[bk]])

        def sm(h):
            lst = []
            for m in range(2):
                bk = 2 * (h % 2) + m
                pt, r_pt = ptr.next()
                act(pt[:, 0:nq], ps[bk][:, 0:nq], AF.Exp, [r_ps[bk]], [r_pt])
                lst.append((pt, r_pt))
            pts[h] = lst

        def pv(h):
            bo, bl = (4, 5) if h % 2 == 0 else (6, 7)
            lst = pts.pop(h)
            for m in range(2):
                pt, r_pt = lst[m]
                mm(ps[bo][:, 0:nq], mv[l][:, m, h * 128:(h + 1) * 128], pt[:, 0:nq], m == 0, m == 1,
                   [r_mv[l], r_pt], [r_ps[bo]])
            for m in range(2):
                pt, r_pt = lst[m]
                mm(ps[bl][:, 0:nq], ones_a[:, :], pt[:, 0:nq], m == 0, m == 1, [r_const, r_pt], [r_ps[bl]])
            tA, r_tA = tmpA.next()
            recip(tA[:, 0:nq], ps[bl][:, 0:nq], [r_ps[bl]], [r_tA])
            tt(mix[:, 12 + h, q0:q0 + nq], ps[bo][:, 0:nq], tA[:, 0:nq], ALU.mult, [r_ps[bo], r_tA], [r_mix[12 + h]])

        sc(0)
        for h in range(NH):
            if h + 1 < NH:
                sc(h + 1)
            sm(h)
            pv(h)

    for blk in range(NBLK):
        if STOP_AT <= 0 or (STOP_AT <= 7 and blk > 0):
            break
        P.tag = 'load'
        load_tokens(xp_d[blk * TBP:(blk + 1) * TBP, :], TBP, True)
        for l in range(L):
            layer_block(l, TBP, [(0, TBP)], blk, False)
    for l in range(L):
        if kstate[l] is not None:
            kfree.append(kstate[l])
            kstate[l] = None
    if STOP_AT > 8 and 'nosample' not in ABL:
        load_tokens(xs_d, 2 * DEC_SEQ, True)
        for l in range(L):
            layer_block(l, 2 * DEC_SEQ, [(0, DEC_SEQ), (DEC_SEQ, DEC_SEQ)], None, True)

    with nc.Block() as block:
        P.emit(block)
    return nc


def _prep_weights(w_in, conv_w, conv_b, conv_ln_g, conv_ln_b, w_pw, rel_table, w_mem_kv, w_out, ln_g, ln_b):
    f = np.float32
    w_in = np.asarray(w_in, f)
    w_in_t = np.ascontiguousarray(w_in.reshape(DEPTH, 16, 128, 48, 128).transpose(0, 3, 2, 1, 4))

    def moving(w):
        return w.reshape(DEPTH, 4, 4, 128, 512).transpose(0, 1, 3, 2, 4)

    w_kv_t = np.ascontiguousarray(np.stack([moving(w_in[:, :, 3584:4096]), moving(w_in[:, :, 4096:4608])], axis=1))
    w_pw_t = np.ascontiguousarray(np.asarray(w_pw, f).reshape(DEPTH, 8, 128, 8, 128).transpose(0, 3, 2, 1, 4))
    w_out_t = np.ascontiguousarray(np.asarray(w_out, f).reshape(DEPTH, 16, 128, 16, 128).transpose(0, 3, 2, 1, 4))
    wm = np.asarray(w_mem_kv, f)
    w_mkv_t = np.ascontiguousarray(np.stack([moving(wm[:, :, 0:512]), moving(wm[:, :, 512:1024])], axis=1))
    convw_t = np.ascontiguousarray(np.asarray(conv_w, f).reshape(DEPTH, CW, 8, 128).transpose(3, 0, 2, 1))
    cw_t = np.asarray(conv_w, f).reshape(DEPTH, CW, 8, 128).transpose(0, 2, 3, 1)
    convd_t = np.zeros((DEPTH, 8, 128, CW, 128), f)
    ar = np.arange(128)
    convd_t[:, :, ar, :, ar] = cw_t.transpose(2, 0, 1, 3)
    cpar = np.stack([np.asarray(conv_b, f), np.asarray(conv_ln_g, f), np.asarray(conv_ln_b, f)], axis=1)
    cpar_t = np.ascontiguousarray(cpar.reshape(DEPTH, 3, 8, 128).transpose(3, 0, 1, 2))
    lpar = np.stack([np.asarray(ln_g, f), np.asarray(ln_b, f)], axis=1)
    lpar_t = np.ascontiguousarray(lpar.reshape(DEPTH, 2, 16, 128).transpose(3, 0, 1, 2))
    kl = np.arange(640)[:, None]
    q = np.arange(128)[None, :]
    idx = np.clip(q - (kl - 512), -128, 128) + 128
    valid = np.where(q < 64, kl < 576, kl >= 64)
    tab = np.asarray(rel_table, f)
    b = tab[:, :, idx]
    b = np.where(valid[None, None], b, f(-1e30))
    bias_t = np.ascontiguousarray(b.reshape(DEPTH, NH, 5, 128, 128).transpose(0, 3, 1, 2, 4))
    return dict(w_in_t=w_in_t, w_kv_t=w_kv_t, w_pw_t=w_pw_t, w_out_t=w_out_t, w_mkv_t=w_mkv_t, convw_t=convw_t,
                convd_t=convd_t, cpar_t=cpar_t, lpar_t=lpar_t, bias_t=bias_t, ident=np.eye(128, dtype=f))


_NC_CACHE = {}


def kernel(x_prompt, x_sample, mem_prompt, cache_conv, cache_att_k, cache_att_v, cache_mem_k, cache_mem_v,
           w_in, conv_w, conv_b, conv_ln_g, conv_ln_b, w_pw, rel_table, w_mem_kv, w_out, ln_g, ln_b):
    f = np.float32
    shared = _prep_weights(w_in, conv_w, conv_b, conv_ln_g, conv_ln_b, w_pw, rel_table, w_mem_kv, w_out, ln_g, ln_b)
    x_prompt = np.asarray(x_prompt, f)
    x_sample = np.asarray(x_sample, f)
    mem_prompt = np.asarray(mem_prompt, f)
    cache_conv = np.asarray(cache_conv, f)
    cak = np.asarray(cache_att_k, f).reshape(DEPTH, 16, 512, 512)
    cav = np.asarray(cache_att_v, f).reshape(DEPTH, 16, 512, 512)
    cmk = np.asarray(cache_mem_k, f).reshape(DEPTH, 16, NMEM, 512)
    cmv = np.asarray(cache_mem_v, f).reshape(DEPTH, 16, NMEM, 512)
    in_maps = []
    for i in range(NCORES):
        m = dict(shared)
        m["xp"] = np.ascontiguousarray(x_prompt[i])
        m["xs"] = np.ascontiguousarray(x_sample[2 * i:2 * i + 2].reshape(2 * DEC_SEQ, D))
        m["memp"] = np.ascontiguousarray(mem_prompt[i])
        m["cconv"] = np.ascontiguousarray(cache_conv[:, 2 * i:2 * i + 2])
        m["catk"] = np.ascontiguousarray(cak[:, 2 * i:2 * i + 2])
        m["catv"] = np.ascontiguousarray(cav[:, 2 * i:2 * i + 2])
        m["cmk"] = np.ascontiguousarray(cmk[:, 2 * i:2 * i + 2])
        m["cmv"] = np.ascontiguousarray(cmv[:, 2 * i:2 * i + 2])
        in_maps.append(m)
    if "nc" not in _NC_CACHE:
        _NC_CACHE["nc"] = build_program()
    nc = _NC_CACHE["nc"]
    res = run_bass_kernel_spmd(nc, in_maps, core_ids=list(range(NCORES)))
    R = res.results
    y_p = np.stack([R[i]["y_p"] for i in range(NCORES)], axis=0)
    y_s = np.stack([R[i]["y_s"] for i in range(NCORES)], axis=0).reshape(16, DEC_SEQ, D)
    ncp = np.stack([R[i]["ncp"] for i in range(NCORES)], axis=1)
    nkp = np.stack([R[i]["nkp"] for i in range(NCORES)], axis=1).reshape(DEPTH, 8, 512, NH, HD)
    nvp = np.stack([R[i]["nvp"] for i in range(NCORES)], axis=1).reshape(DEPTH, 8, 512, NH, HD)
    nmk = np.stack([R[i]["nmk"] for i in range(NCORES)], axis=1).reshape(DEPTH, 8, NMEM, NH, HD)
    nmv = np.stack([R[i]["nmv"] for i in range(NCORES)], axis=1).reshape(DEPTH, 8, NMEM, NH, HD)
    ncs = np.concatenate([R[i]["ncs"] for i in range(NCORES)], axis=1)
    nks = np.concatenate([R[i]["nks"] for i in range(NCORES)], axis=1).reshape(DEPTH, 16, DEC_SEQ, NH, HD)
    nvs = np.concatenate([R[i]["nvs"] for i in range(NCORES)], axis=1).reshape(DEPTH, 16, DEC_SEQ, NH, HD)
    return (y_p.astype(f), y_s.astype(f), ncp.astype(f), nkp.astype(f), nvp.astype(f), nmk.astype(f),
            nmv.astype(f), ncs.astype(f), nks.astype(f), nvs.astype(f))
```
